# Optimizing a Trainium2 kernel written in Bass

```python
import jax, jax.numpy as jnp
from jax import lax
import numpy as np

D_MODEL = 1024
BATCH = 2
SEQ = 8192
DEPTH = 2
DEC_BATCH = 128
DEC_SEQ = 1
PAST_LEN = 8192
PAGE_SIZE = 128

D_CONV = D_MODEL // 2
CONV_W = 3
HEAD_DIM = 64
N_HEADS = (D_MODEL // 2) // HEAD_DIM
N_KV_HEADS = 2
GROUP = N_HEADS // N_KV_HEADS
WINDOW = 128
BLK = WINDOW
D_ATTN = N_HEADS * HEAD_DIM
D_KV = N_KV_HEADS * HEAD_DIM
MIX = D_CONV + D_ATTN
IN_COLS = 3 * D_CONV + D_ATTN + 2 * D_KV
MEM_LEN = 256
MEM_HEADS = 4
MEM_HEAD_DIM = D_MODEL // MEM_HEADS
MEM_INNER = MEM_HEADS * MEM_HEAD_DIM
D_FF = 2816
ALPHA = (2.0 * DEPTH) ** 0.25
BETA = (8.0 * DEPTH) ** -0.25
LN_EPS = 1e-5

kernel_name = 'hymba_conv_swa_sink_macaron_deepnorm_step'


def _layer_norm(x, g, b):
    xf = x.astype(jnp.float32)
    mu = xf.mean(-1, keepdims=True)
    var = jnp.square(xf - mu).mean(-1, keepdims=True)
    y = (xf - mu) * lax.rsqrt(var + LN_EPS) * g.astype(jnp.float32) + b.astype(jnp.float32)
    return y.astype(x.dtype)


def _deepnorm(x, sub, g, b):
    return _layer_norm(ALPHA * x + sub, g, b)


def _swiglu_half(x, w_gu, w_down):
    g, u = jnp.split(x @ w_gu, 2, axis=-1)
    return 0.5 * ((jax.nn.silu(g) * u) @ w_down)


def _split_in(p):
    cuts = [D_CONV, 2 * D_CONV, 3 * D_CONV, 3 * D_CONV + D_ATTN, 3 * D_CONV + D_ATTN + D_KV]
    return jnp.split(p, cuts, axis=-1)


def _dwconv(u_pad, w):
    T = u_pad.shape[1] - (CONV_W - 1)
    y = w[0] * u_pad[:, 0:T]
    for tap in range(1, CONV_W):
        y = y + w[tap] * u_pad[:, tap:tap + T]
    return y


def _sink_attend(q, k, v, mask, sinks):
    s = jnp.einsum('...qkgd,...ckd->...kgqc', q.astype(jnp.float32), k.astype(jnp.float32)) * (HEAD_DIM ** -0.5)
    s = jnp.where(mask, s, -jnp.inf)
    sk = sinks.astype(jnp.float32).reshape(N_KV_HEADS, GROUP, 1, 1)
    m = jnp.maximum(s.max(-1, keepdims=True), sk)
    p = jnp.exp(s - m)
    denom = p.sum(-1, keepdims=True) + jnp.exp(sk - m)
    o = jnp.einsum('...kgqc,...ckd->...qkgd', p / denom, v.astype(jnp.float32))
    return o.astype(v.dtype)


def _swa_prompt(q, k, v, sinks):
    Bn, S = q.shape[0], q.shape[1]
    nb = S // BLK
    qb = q.reshape(Bn, nb, BLK, N_KV_HEADS, GROUP, HEAD_DIM)
    kb = k.reshape(Bn, nb, BLK, N_KV_HEADS, HEAD_DIM)
    vb = v.reshape(Bn, nb, BLK, N_KV_HEADS, HEAD_DIM)
    pad = jnp.zeros_like(kb[:, :1])
    kk = jnp.concatenate([jnp.concatenate([pad, kb[:, :-1]], axis=1), kb], axis=2)
    vv = jnp.concatenate([jnp.concatenate([pad, vb[:, :-1]], axis=1), vb], axis=2)
    a = jnp.arange(BLK)[None, :, None]
    c = jnp.arange(2 * BLK)[None, None, :]
    blk = jnp.arange(nb)[:, None, None]
    rel = BLK + a - c
    mask = (rel >= 0) & (rel <= WINDOW) & ((blk - 1) * BLK + c >= 0)
    o = _sink_attend(qb, kk, vv, mask[None, :, None, None], sinks)
    return o.reshape(Bn, S, D_ATTN)


def _swa_sample(q, k_new, v_new, k_buf, v_buf, sinks):
    Bn, T = q.shape[0], q.shape[1]
    kk = jnp.concatenate([k_buf, k_new], axis=1)
    vv = jnp.concatenate([v_buf, v_new], axis=1)
    a = jnp.arange(T)[:, None]
    c = jnp.arange(WINDOW + T)[None, :]
    rel = WINDOW + a - c
    mask = (rel >= 0) & (rel <= WINDOW)
    o = _sink_attend(q.reshape(Bn, T, N_KV_HEADS, GROUP, HEAD_DIM), kk, vv, mask, sinks)
    return o.reshape(Bn, T, D_ATTN), kk[:, -WINDOW:], vv[:, -WINDOW:]


def _token_mix(x, w_in, conv_w, w_out, sinks, conv_prev, k_buf, v_buf):
    Bn, T = x.shape[0], x.shape[1]
    bg, cg, hc, q, k, v = _split_in(x @ w_in)
    u = cg * hc
    u_pad = jnp.concatenate([conv_prev, u], axis=1)
    z_conv = bg * _dwconv(u_pad, conv_w)
    new_conv = u_pad[:, -(CONV_W - 1):]
    k4 = k.reshape(Bn, T, N_KV_HEADS, HEAD_DIM)
    v4 = v.reshape(Bn, T, N_KV_HEADS, HEAD_DIM)
    if k_buf is None:
        z_attn = _swa_prompt(q, k4, v4, sinks)
        new_k, new_v = k4[:, -WINDOW:], v4[:, -WINDOW:]
    else:
        z_attn, new_k, new_v = _swa_sample(q, k4, v4, k_buf, v_buf, sinks)
    out = jnp.concatenate([z_conv, z_attn], axis=-1) @ w_out
    return out, new_conv, new_k, new_v


def _mem_kv(mem, w_mk, w_mv):
    Bn, M = mem.shape[0], mem.shape[1]
    mk = (mem @ w_mk).reshape(Bn, M, MEM_HEADS, MEM_HEAD_DIM)
    mv = (mem @ w_mv).reshape(Bn, M, MEM_HEADS, MEM_HEAD_DIM)
    return mk, mv


def _cross_attend(x, mk, mv, w_cq, w_co):
    Bn, T = x.shape[0], x.shape[1]
    q = (x @ w_cq).reshape(Bn, T, MEM_HEADS, MEM_HEAD_DIM)
    s = jnp.einsum('bthd,bmhd->bhtm', q.astype(jnp.float32), mk.astype(jnp.float32)) * (MEM_HEAD_DIM ** -0.5)
    p = jax.nn.softmax(s, axis=-1)
    o = jnp.einsum('bhtm,bmhd->bthd', p, mv.astype(jnp.float32)).astype(x.dtype)
    return o.reshape(Bn, T, MEM_INNER) @ w_co


def setup_inputs(seed: int = 0) -> dict:
    key = jax.random.key(seed)
    ks = jax.random.split(key, 24)
    f32 = jnp.float32
    nrm = lambda k, shape, scale: jax.random.normal(k, shape, f32) * scale
    w_in = nrm(ks[10], (DEPTH, D_MODEL, IN_COLS), D_MODEL ** -0.5)
    w_in = w_in.at[..., -D_KV:].multiply(BETA)
    return {
        'x_prompt': nrm(ks[0], (BATCH, SEQ, D_MODEL), 1.0),
        'x_sample': nrm(ks[1], (DEC_BATCH, DEC_SEQ, D_MODEL), 1.0),
        'mem_prompt': nrm(ks[2], (BATCH, MEM_LEN, D_MODEL), 1.0),
        'cache_win_k': nrm(ks[3], (DEPTH, DEC_BATCH, WINDOW, N_KV_HEADS, HEAD_DIM), 1.0),
        'cache_win_v': nrm(ks[4], (DEPTH, DEC_BATCH, WINDOW, N_KV_HEADS, HEAD_DIM), BETA),
        'state_conv': nrm(ks[5], (DEPTH, DEC_BATCH, CONV_W - 1, D_CONV), 1.0),
        'cache_mem_k': nrm(ks[6], (DEPTH, DEC_BATCH, MEM_LEN, MEM_HEADS, MEM_HEAD_DIM), 1.0),
        'cache_mem_v': nrm(ks[7], (DEPTH, DEC_BATCH, MEM_LEN, MEM_HEADS, MEM_HEAD_DIM), BETA),
        'ln_g': 1.0 + nrm(ks[8], (DEPTH, 4, D_MODEL), 0.02),
        'ln_b': nrm(ks[9], (DEPTH, 4, D_MODEL), 0.02),
        'ffn1_w_gu': nrm(ks[11], (DEPTH, D_MODEL, 2 * D_FF), D_MODEL ** -0.5),
        'ffn1_w_down': nrm(ks[12], (DEPTH, D_FF, D_MODEL), BETA * D_FF ** -0.5),
        'w_in': w_in,
        'conv_w': nrm(ks[13], (DEPTH, CONV_W, D_CONV), CONV_W ** -0.5),
        'attn_sinks': nrm(ks[14], (DEPTH, N_HEADS), 0.5),
        'w_out': nrm(ks[15], (DEPTH, MIX, D_MODEL), BETA * MIX ** -0.5),
        'w_cq': nrm(ks[16], (DEPTH, D_MODEL, MEM_INNER), D_MODEL ** -0.5),
        'w_mk': nrm(ks[17], (DEPTH, D_MODEL, MEM_INNER), D_MODEL ** -0.5),
        'w_mv': nrm(ks[18], (DEPTH, D_MODEL, MEM_INNER), BETA * D_MODEL ** -0.5),
        'w_co': nrm(ks[19], (DEPTH, MEM_INNER, D_MODEL), BETA * MEM_INNER ** -0.5),
        'ffn2_w_gu': nrm(ks[20], (DEPTH, D_MODEL, 2 * D_FF), D_MODEL ** -0.5),
        'ffn2_w_down': nrm(ks[21], (DEPTH, D_FF, D_MODEL), BETA * D_FF ** -0.5),
    }


def reference(x_prompt, x_sample, mem_prompt, cache_win_k, cache_win_v, state_conv, cache_mem_k, cache_mem_v,
              ln_g, ln_b, ffn1_w_gu, ffn1_w_down, w_in, conv_w, attn_sinks, w_out,
              w_cq, w_mk, w_mv, w_co, ffn2_w_gu, ffn2_w_down):
    yp, ys = x_prompt, x_sample
    wkp, wvp, cvp, mkp, mvp = [], [], [], [], []
    wks, wvs, cvs = [], [], []
    for l in range(DEPTH):
        yp = _deepnorm(yp, _swiglu_half(yp, ffn1_w_gu[l], ffn1_w_down[l]), ln_g[l, 0], ln_b[l, 0])
        ys = _deepnorm(ys, _swiglu_half(ys, ffn1_w_gu[l], ffn1_w_down[l]), ln_g[l, 0], ln_b[l, 0])
        conv0 = jnp.zeros((yp.shape[0], CONV_W - 1, D_CONV), yp.dtype)
        mix_p, cst_p, kp, vp = _token_mix(yp, w_in[l], conv_w[l], w_out[l], attn_sinks[l], conv0, None, None)
        mix_s, cst_s, ksn, vsn = _token_mix(ys, w_in[l], conv_w[l], w_out[l], attn_sinks[l],
                                            state_conv[l], cache_win_k[l], cache_win_v[l])
        yp = _deepnorm(yp, mix_p, ln_g[l, 1], ln_b[l, 1])
        ys = _deepnorm(ys, mix_s, ln_g[l, 1], ln_b[l, 1])
        wkp.append(kp); wvp.append(vp); cvp.append(cst_p)
        wks.append(ksn); wvs.append(vsn); cvs.append(cst_s)
        mk, mv = _mem_kv(mem_prompt, w_mk[l], w_mv[l])
        mkp.append(mk); mvp.append(mv)
        yp = _deepnorm(yp, _cross_attend(yp, mk, mv, w_cq[l], w_co[l]), ln_g[l, 2], ln_b[l, 2])
        ys = _deepnorm(ys, _cross_attend(ys, cache_mem_k[l], cache_mem_v[l], w_cq[l], w_co[l]), ln_g[l, 2], ln_b[l, 2])
        yp = _deepnorm(yp, _swiglu_half(yp, ffn2_w_gu[l], ffn2_w_down[l]), ln_g[l, 3], ln_b[l, 3])
        ys = _deepnorm(ys, _swiglu_half(ys, ffn2_w_gu[l], ffn2_w_down[l]), ln_g[l, 3], ln_b[l, 3])
    new_win_k_prompt = jnp.stack(wkp)
    new_win_v_prompt = jnp.stack(wvp)
    new_conv_prompt = jnp.stack(cvp)
    new_mem_k_prompt = jnp.stack(mkp)
    new_mem_v_prompt = jnp.stack(mvp)
    new_win_k_sample = jnp.stack(wks)
    new_win_v_sample = jnp.stack(wvs)
    new_conv_sample = jnp.stack(cvs)
    return (yp, ys, new_win_k_prompt, new_win_v_prompt, new_conv_prompt, new_mem_k_prompt, new_mem_v_prompt,
            new_win_k_sample, new_win_v_sample, new_conv_sample)
```

```python
import contextlib
import numpy as np
import concourse.bass as bass
import concourse.mybir as mybir
from concourse.bass_utils import run_bass_kernel_spmd

F32 = mybir.dt.float32
BF16 = mybir.dt.bfloat16
AF = mybir.ActivationFunctionType
ALU = mybir.AluOpType
AX = mybir.AxisListType

ENGS = ("pe", "act", "dve", "pool", "sp")

D = 1024
KC = 8
FFD = 2816
FC = 22
NSPLIT = 2
FH = FC // NSPLIT
L = 2
NPT = 384
NS = 16
NT = 3
NG = 2
GP = NPT * NT
GT = GP + NS
TTM = NPT + NS
NCOLS = NG * GP + NS
HALO = 256
OWN = 2048
ALPHA = (2.0 * L) ** 0.25
EPS = 1e-5
NEG = -30000.0
RING = 3
LOOK = 1
import os
KSTOP = int(os.environ.get('KSTOP', '99'))
KCROSS = int(os.environ.get('KCROSS', '9'))


class Res:
    __slots__ = ("name", "w", "r")

    def __init__(self, name):
        self.name = name
        self.w = None
        self.r = []


class Prog:
    def __init__(self):
        self.ops = {e: [] for e in ENGS}
        self.cnt = {}
        self.seen = {e: {} for e in ENGS}
        self.semkeys = []
        self.deferred = {}

    def _bump(self, key, inc):
        if key not in self.cnt:
            self.cnt[key] = 0
            self.semkeys.append(key)
        self.cnt[key] += inc
        return (key, self.cnt[key])

    def op(self, eng, fn, reads=(), writes=(), dma=None, ndma=1, arena=False):
        if arena and self.deferred.get(eng):
            self.ops[eng].append((None, self.deferred[eng], None))
            self.deferred[eng] = []
        deps = set()
        for r in reads:
            if r.w is not None:
                deps.add(r.w)
        for w in writes:
            if w.w is not None:
                deps.add(w.w)
            deps.update(w.r)
        best = {}
        for (k, v) in deps:
            if eng == "pe" and k == "e:pe":
                continue
            if v > best.get(k, 0):
                best[k] = v
        waits = []
        seen = self.seen[eng]
        for k, v in best.items():
            if seen.get(k, 0) >= v:
                continue
            seen[k] = v
            waits.append((k, v))
        if dma is not None:
            tok = self._bump("d:" + dma, 16 * ndma)
        else:
            tok = self._bump("e:" + eng, 1)
        self.ops[eng].append((fn, waits, tok))
        for r in reads:
            r.r.append(tok)
        for w in writes:
            w.w = tok
            w.r = []
        return tok

    def barrier(self, full=False):
        allt = dict(self.cnt)
        for e in ENGS:
            if e == "pe" and not full:
                continue
            if e in ("pool", "sp") and not full:
                d = dict(self.deferred.get(e) or [])
                for k, v in allt.items():
                    if v > 0 and v > d.get(k, 0):
                        d[k] = v
                self.deferred[e] = list(d.items())
                continue
            waits = []
            for k, v in allt.items():
                if v > 0 and self.seen[e].get(k, 0) < v:
                    self.seen[e][k] = v
                    waits.append((k, v))
            if waits:
                self.ops[e].append((None, waits, None))

    def emit(self, nc):
        with contextlib.ExitStack() as st:
            sems = {}
            for k in self.semkeys:
                sems[k] = st.enter_context(nc.semaphore(k.replace(":", "_")))
            fw = [(k, v) for k, v in self.cnt.items() if v > 0]
            self.ops["sp"].append((None, fw, None))
            block = st.enter_context(nc.Block())

            def run(engname):
                def body(eng):
                    for (fn, waits, tok) in self.ops[engname]:
                        for (k, v) in waits:
                            eng.wait_ge(sems[k], v)
                        if fn is None:
                            continue
                        ins = fn(eng)
                        if tok[0].startswith("d:"):
                            for i_ in (ins if isinstance(ins, (list, tuple)) else [ins]):
                                i_.then_inc(sems[tok[0]], 16)
                        else:
                            if isinstance(ins, (list, tuple)):
                                ins = ins[-1]
                            ins.then_inc(sems[tok[0]], 1)
                return body

            block.tensor(run("pe"))
            block.scalar(run("act"))
            block.vector(run("dve"))
            block.gpsimd(run("pool"))
            block.sync(run("sp"))


def gcol(l, i, k):
    return ((l * 4 + i) * 8 + k) * 2


def build_nc():
    nc = bass.Bass("TRN2", target_bir_lowering=False)

    def din(name, shape):
        return nc.dram_tensor(name, list(shape), F32, kind="ExternalInput").ap()

    def dout(name, shape):
        return nc.dram_tensor(name, list(shape), F32, kind="ExternalOutput").ap()

    xT = din("xT", [D, NCOLS])
    memT = din("memT", [D, 256])
    w_gu = [din("w_gu1", [L, D, 2 * FFD]), din("w_gu2", [L, D, 2 * FFD])]
    w_dn = [din("w_dn1", [L, FFD, D]), din("w_dn2", [L, FFD, D])]
    w_in = din("w_in", [L, D, 2304])
    w_out = din("w_out", [L, D, D])
    w_cq = din("w_cq", [L, D, D])
    w_mk = din("w_mk", [L, D, D])
    w_mv = din("w_mv", [L, D, D])
    w_co = din("w_co", [L, D, D])
    gvec = din("gvec", [128, L * 4 * 8 * 2])
    convw = din("convw", [128, L * 3 * 4])
    sinkq = din("sinkq", [128, L * 8])
    sinks = din("sinks", [128, L])
    ident = din("ident", [128, 128])
    maskA = din("maskA", [128, 256])
    maskF = din("maskF", [128, 256])
    flag = din("flag", [128, 1])
    stT = din("stT", [128, L * 4 * 2 * NS])
    wk_o = din("wk_o", [L, NS, 128, 128])
    wv_o = din("wv_o", [L, NS, 128, 128])
    wk_s = din("wk_s", [L, 128, 8192])
    wv_s = din("wv_s", [L, 128, 8192])
    cmk = din("cmk", [L, 128, 32768])
    cmv = din("cmv", [L, 128, 32768])

    yT = dout("yT", [D, OWN + NS])
    wkp = dout("wkp", [L, 128, 128])
    wvp = dout("wvp", [L, 128, 128])
    convp = dout("convp", [L, 4, 128, 2])
    mkp = dout("mkp", [L, D, 256])
    mvp = dout("mvp", [L, 256, D])
    wks = dout("wks", [L, NS, 128, 128])
    wvs = dout("wvs", [L, NS, 128, 128])
    convs = dout("convs", [L, 4, 128, 2, NS])

    def dscr(name, shape):
        return nc.dram_tensor(name, list(shape), F32).ap()

    scr_q = dscr("scr_q", [NS, 512])
    scr_k = dscr("scr_k", [NS, 512])
    scr_v = dscr("scr_v", [NS, 512])
    scr_o = dscr("scr_o", [NS, 512])
    scr_qc = dscr("scr_qc", [NS, 4, 2, 256])
    scr_sc = dscr("scr_sc", [128, 128])
    scr_pc = dscr("scr_pc", [128, 128])
    scr_oc = dscr("scr_oc", [128, 256])
    scr_mkt = nc.dram_tensor("scr_mkt", [L, 128, KC * 256], BF16).ap()
    scr_mv = nc.dram_tensor("scr_mv", [L, 128, 2 * D], BF16).ap()

    def sb(name, shape, dt=F32):
        return nc.alloc_sbuf_tensor(name, list(shape), dt)

    X = sb("X", [128, KC, GT])
    Xb = sb("Xb", [128, KC, GT], BF16)
    ring = [sb("ring%d" % i, [128, KC, 512], BF16) for i in range(RING)]
    WVX = sb("WVX", [128, KC, 512], BF16)
    yb = sb("yb", [128, KC, TTM], BF16)
    ysq = sb("ysq", [128, KC, TTM], BF16)
    mean = sb("mean", [128, TTM])
    var = sb("var", [128, TTM])
    rstd = sb("rstd", [128, TTM])
    nmr = sb("nmr", [128, TTM])
    t1 = [sb("t1_%d" % i, [128, TTM]) for i in range(2)]
    sg = [sb("sg_%d" % i, [128, TTM]) for i in range(2)]
    identb = sb("identb", [128, 128], BF16)
    identf = sb("identf", [128, 128])
    onesb = sb("onesb", [128, 128], BF16)
    maskAb = sb("maskAb", [128, 256], BF16)
    maskFb = sb("maskFb", [128, 256], BF16)
    gv = sb("gv", [128, L * 4 * 8 * 2])
    cw = sb("cw", [128, L * 3 * 4])
    sq = sb("sq", [128, L * 8])
    ss = sb("ss", [128, L])
    flg = sb("flg", [128, 1])
    epsc = sb("epsc", [128, 1])
    nsq = sb("nsq", [128, L * 8])
    stS = sb("stS", [128, L * 4 * 2 * NS])
    MKT = sb("MKT", [128, KC, 256], BF16)
    MV = sb("MV", [128, 2, D], BF16)
    KKc = [sb("KKc%d" % l, [128, 2, 128], BF16) for l in range(L)]
    Vpc = [sb("Vpc%d" % l, [128, 512], BF16) for l in range(L)]
    Uc = [sb("Uc%d" % l, [128, 4, 2]) for l in range(L)]
    small = sb("small", [128, 64])
    stg = [sb("stg%d" % i, [128, 512]) for i in range(2)]
    qx = sb("qx", [128, 256])
    ssm = sb("ssm", [128, 258])
    oacc = sb("oacc", [128, 256])
    AR = 17696
    arena = sb("arena", [128, AR])

    def aview(off_words, nwords, dt, pattern=None, **kw):
        v = arena[:, off_words:off_words + nwords]
        if dt == BF16:
            v = v.bitcast(BF16)
        if pattern:
            v = v.rearrange(pattern, **kw)
        return v

    H = aview(0, FH * GT // 2, BF16, "p (j t) -> p j t", j=FH)
    o_ = FH * GT // 2
    WD = [aview(o_ + i * (FH * D // 2), FH * D // 2, BF16, "p (j c) -> p j c", j=FH) for i in range(2)]
    assert o_ + 2 * (FH * D // 2) <= AR
    o_ = 0
    Zb = aview(o_, KC * TTM // 2, BF16, "p (c t) -> p c t", c=KC); o_ += KC * TTM // 2
    U = aview(o_, 4 * (TTM + 2), F32, "p (c t) -> p c t", c=4); oU = o_; o_ += 4 * (TTM + 2)
    QT = aview(o_, 4 * TTM // 2, BF16, "p (c t) -> p c t", c=4); o_ += 4 * TTM // 2
    QcT = aview(oU, KC * TTM // 2, BF16, "p (c t) -> p c t", c=KC)
    assert KC * TTM // 2 <= 4 * (TTM + 2) + 4 * TTM // 2
    KK = aview(o_, 2 * (128 + TTM) // 2, BF16, "p (c t) -> p c t", c=2); o_ += 2 * (128 + TTM) // 2
    Vp = aview(o_, 4 * 512 // 2, BF16, "p (c t) -> p c t", c=4); o_ += 4 * 512 // 2
    Pm = aview(o_, 8 * 256 // 2, BF16, "p (c t) -> p c t", c=8); o_ += 8 * 256 // 2
    PT = aview(o_, 16 * 128 // 2, BF16, "p (c t) -> p c t", c=16); o_ += 16 * 128 // 2
    hcs = aview(o_, TTM, F32); o_ += TTM
    acc = aview(o_, TTM, F32); o_ += TTM
    SBo = o_
    SB = aview(o_, 8256, F32); o_ += 8256
    assert o_ <= AR, o_
    KX = SB.rearrange("p (c d) -> p c d", d=64)
    tokS = SB[0:NS, 0:2048]
    SBh = [aview(SBo + i * 4096, 4096, F32, "p (m d) -> p m d", d=256) for i in range(2)]

    ps = nc.alloc_psum_tensor("ps", [128, 6, 512], F32)
    pst2 = nc.alloc_psum_tensor("pst", [128, 2, 1024], BF16)

    R = {}

    def res(n):
        if n not in R:
            R[n] = Res(n)
        return R[n]

    rX = [[res("X%d_%d" % (t, m)) for m in range(KC)] for t in range(NT)]
    rXb = [res("Xb%d" % t) for t in range(NT)]
    rring = [res("ring%d" % i) for i in range(RING)]
    rps = [res("ps%d" % i) for i in range(6)]
    rpstl = [res("pst0"), res("pst1")]
    rH = [res("H%d" % t) for t in range(NT)]
    rWD = [res("WD%d" % i) for i in range(2)]
    rt1 = [res("t1_%d" % i) for i in range(2)]
    rsg = [res("sg_%d" % i) for i in range(2)]
    rstg = [res("stg%d" % i) for i in range(2)]
    rSB = [res("SB0"), res("SB1")]
    R["tokS"] = rSB[0]

    state = {"reqs": None, "rec": [], "idx": 0, "issued": 0, "bank": 0, "t1": 0, "sg": 0, "stg": 0, "pend": [], "pstat": [], "trim": 0}

    def run_build(P, reqs):
        state.update(reqs=reqs, rec=[], idx=0, issued=0, bank=0, t1=0, sg=0, stg=0, pend=[], pstat=[], trim=0)
        for r_ in R.values():
            r_.w = None
            r_.r = []

        def issue_load(j):
            parts = state["reqs"][j]
            slot = j % RING

            def fn(e, parts=parts, slot=slot):
                out = []
                for (dst0, ncol, src) in parts:
                    out.append(e.dma_start(out=ring[slot][:, :, dst0:dst0 + ncol],
                                           in_=src.rearrange("(k p) c -> p k c", p=128)))
                return out
            P.op("pool", fn, writes=[rring[slot]], dma="ring%d" % slot, ndma=len(parts))

        def getw(parts):
            i = state["idx"]
            state["idx"] += 1
            state["rec"].append(parts)
            if state["reqs"] is not None:
                while state["issued"] < min(i + 1 + LOOK, len(state["reqs"])):
                    issue_load(state["issued"])
                    state["issued"] += 1
            return ring[i % RING], rring[i % RING]

        def bank():
            b = state["bank"]
            state["bank"] = (b + 1) % 4
            return b

        def nxt(key, n=2):
            v = state[key]
            state[key] = (v + 1) % n
            return v

        def tile_cols(g, t):
            n = NPT + (NS if (g == NG - 1 and t == NT - 1) else 0)
            if g == 0 and t == 0 and state["trim"]:
                return state["trim"], n - state["trim"]
            return t * NPT, n

        def ld(eng, dst, src, r_, key):
            P.op(eng, lambda e: e.dma_start(out=dst, in_=src), writes=[r_], dma=key)

        ld("pool", identb[:], ident, res("identb"), "c0")
        ld("sp", identf[:], ident, res("identf"), "c1")
        ld("pool", maskAb[:], maskA, res("maskAb"), "c2")
        ld("pool", maskFb[:], maskF, res("maskFb"), "c3")
        ld("sp", gv[:], gvec, res("gv"), "c4")
        ld("sp", cw[:], convw, res("cw"), "c5")
        ld("sp", sq[:], sinkq, res("sq"), "c6")
        ld("sp", ss[:], sinks, res("ss"), "c7")
        ld("sp", flg[:], flag, res("flg"), "c8")
        ld("sp", stS[:], stT, res("stS"), "c9")
        P.op("dve", lambda e: e.memset(onesb[:], 1.0 / D), writes=[res("onesb")])
        P.op("dve", lambda e: e.memset(epsc[:], EPS), writes=[res("epsc")])
        P.op("dve", lambda e: e.tensor_scalar(out=nsq[:], in0=sq[:], scalar1=-1.0, scalar2=None, op0=ALU.mult), reads=[res("sq")], writes=[res("nsq")])
        P.op("dve", lambda e: e.memset(WVX[:], 0.0), writes=[res("WVX")])
        shv = arena[:, 0:2048].rearrange("p (b x) -> p b x", x=128)
        for l in range(L):
            for (src, dst, nm) in ((wk_o, wks, "k"), (wv_o, wvs, "v")):
                P.op("sp", lambda e, src=src, l=l: e.dma_start(out=shv, in_=src[l].rearrange("b k x -> k b x")),
                     writes=[res("shv")], dma="shin", arena=True)
                P.op("sp", lambda e, dst=dst, l=l: e.dma_start(out=dst[l][:, 0:127, :].rearrange("b k x -> k b x"), in_=shv[1:128, :, :]),
                     reads=[res("shv")], writes=[res("o_w%s%d" % (nm, l))], dma="shout")
        P.barrier(full=True)

        def drain(k=1):
            while k > 0 and state["pend"]:
                _, fn_ = state["pend"].pop(0)
                fn_()
                k -= 1

        def need(t):
            if any(tt == t for tt, _ in state["pend"]):
                drain(10 ** 6)

        def ln_piece(g, t, m):
            c0, n = tile_cols(g, t)
            sl = slice(c0, c0 + n)
            P.op("act", lambda e: e.activation(out=yb[:, m, 0:n], in_=X[:, m, sl], func=AF.Copy),
                 reads=[rX[t][m]], writes=[res("yb%d" % m)])
            P.op("act", lambda e: e.activation(out=ysq[:, m, 0:n], in_=X[:, m, sl], func=AF.Square),
                 reads=[rX[t][m]], writes=[res("ysq%d" % m)])

            def stats(e):
                e.matmul(ps[:, 4, 0:n], lhsT=onesb[:], rhs=yb[:, m, 0:n], start=(m == 0), stop=(m == KC - 1))
                return e.matmul(ps[:, 5, 0:n], lhsT=onesb[:], rhs=ysq[:, m, 0:n], start=(m == 0), stop=(m == KC - 1))
            state["pstat"].append(lambda: P.op("pe", stats, reads=[res("onesb"), res("yb%d" % m), res("ysq%d" % m)], writes=[rps[4], rps[5]]))
            while len(state["pstat"]) > 2:
                state["pstat"].pop(0)()

        def layer_norm(g, t, li):
            while state["pstat"]:
                state["pstat"].pop(0)()
            drain(10 ** 6)
            c0, n = tile_cols(g, t)
            sl = slice(c0, c0 + n)
            P.op("dve", lambda e: e.tensor_copy(out=mean[:, 0:n], in_=ps[:, 4, 0:n]), reads=[rps[4]], writes=[res("mean")])
            P.op("dve", lambda e: e.tensor_tensor(out=var[:, 0:n], in0=mean[:, 0:n], in1=mean[:, 0:n], op=ALU.mult),
                 reads=[res("mean")], writes=[res("var")])
            P.op("dve", lambda e: e.tensor_tensor(out=var[:, 0:n], in0=ps[:, 5, 0:n], in1=var[:, 0:n], op=ALU.subtract),
                 reads=[rps[5], res("var")], writes=[res("var")])
            P.op("act", lambda e: e.activation(out=var[:, 0:n], in_=var[:, 0:n], func=AF.Ln, bias=epsc[:, 0:1]), reads=[res("var"), res("epsc")], writes=[res("var")])
            P.op("act", lambda e: e.activation(out=rstd[:, 0:n], in_=var[:, 0:n], func=AF.Exp, scale=-0.5), reads=[res("var")], writes=[res("rstd")])
            l_, i_ = li

            def apply(m):
                b_ = nxt("t1")
                gc = gcol(l_, i_, m)
                P.op("dve", lambda e: e.tensor_tensor(out=t1[b_][:, 0:n], in0=X[:, m, sl], in1=mean[:, 0:n], op=ALU.subtract),
                     reads=[rX[t][m], res("mean")], writes=[rt1[b_]])
                P.op("dve", lambda e: e.tensor_tensor(out=t1[b_][:, 0:n], in0=t1[b_][:, 0:n], in1=rstd[:, 0:n], op=ALU.mult),
                     reads=[rt1[b_], res("rstd")], writes=[rt1[b_]])
                P.op("act", lambda e: e.activation(out=X[:, m, sl], in_=t1[b_][:, 0:n], func=AF.Identity,
                                                   bias=gv[:, gc + 1:gc + 2], scale=gv[:, gc:gc + 1]),
                     reads=[rt1[b_], res("gv")], writes=[rX[t][m]])
                P.op("act", lambda e: e.activation(out=Xb[:, m, sl], in_=t1[b_][:, 0:n], func=AF.Identity,
                                                   bias=gv[:, gc + 1:gc + 2], scale=gv[:, gc:gc + 1]),
                     reads=[rt1[b_], res("gv")], writes=[rXb[t]])
            for m in range(KC):
                state["pend"].append((t, (lambda m=m: apply(m))))

        def proj_chunk(wt, wr, wcol, g, t, extra_reads=(), rhs_src=None, rhs_res=None):
            c0, n = tile_cols(g, t)
            b = bank()
            src = Xb if rhs_src is None else rhs_src
            off = c0 if rhs_src is None else 0

            def fn(e):
                for k in range(KC):
                    ins = e.matmul(ps[:, b, 0:n], lhsT=wt[:, k, wcol:wcol + 128], rhs=src[:, k, off:off + n],
                                   start=(k == 0), stop=(k == KC - 1))
                return ins
            need(t)
            P.op("pe", fn, reads=[wr, rXb[t] if rhs_res is None else rhs_res] + list(extra_reads), writes=[rps[b]])
            drain(1)
            return b

        def resid_evac(g, t, m, b, first=True, final=True):
            need(t)
            c0, n = tile_cols(g, t)
            sl = slice(c0, c0 + n)
            if first:
                P.op("dve", lambda e: e.scalar_tensor_tensor(out=X[:, m, sl], in0=X[:, m, sl], scalar=ALPHA, in1=ps[:, b, 0:n],
                                                             op0=ALU.mult, op1=ALU.add),
                     reads=[rps[b], rX[t][m]], writes=[rX[t][m]])
            else:
                P.op("dve", lambda e: e.tensor_tensor(out=X[:, m, sl], in0=X[:, m, sl], in1=ps[:, b, 0:n], op=ALU.add),
                     reads=[rps[b], rX[t][m]], writes=[rX[t][m]])
            if final:
                ln_piece(g, t, m)

        def ffn(g, l, which):
            state["trim"] = {(0, 0): 0, (0, 1): 128, (1, 0): 128, (1, 1): 256}[(l, which)] if g == 0 else 0
            try:
                ffn_body(g, l, which)
            finally:
                state["trim"] = 0

        def ffn_body(g, l, which):
            wgu = w_gu[which][l]
            wdn = w_dn[which][l]
            for half in range(NSPLIT):
                j0 = half * FH
                def wdload(e, half=half, j0=j0):
                    outs = []
                    for (a, b_) in ((0, 3), (3, 6), (6, 9), (9, FH)):
                        outs.append(e.dma_start(out=WD[half][:, a:b_, :],
                                                in_=wdn[(j0 + a) * 128:(j0 + b_) * 128, :].rearrange("(j p) c -> p j c", p=128)))
                    return outs
                P.op("pool", wdload, writes=[rWD[half]], dma="wd%d" % half, ndma=4, arena=True)
                jj = 0
                while jj < FH:
                    nch = min(2, FH - jj)
                    ca = (j0 + jj) * 128
                    wt, wr = getw([(0, nch * 128, wgu[:, ca:ca + nch * 128]),
                                   (256, nch * 128, wgu[:, FFD + ca:FFD + ca + nch * 128])])
                    for t in range(NT):
                        c0, n = tile_cols(g, t)
                        for cc in range(nch):
                            bg_ = proj_chunk(wt, wr, cc * 128, g, t)
                            bu_ = proj_chunk(wt, wr, 256 + cc * 128, g, t)
                            s_ = nxt("sg")
                            P.op("act", lambda e, s_=s_, bg_=bg_, n=n: e.activation(out=sg[s_][:, 0:n], in_=ps[:, bg_, 0:n], func=AF.Silu),
                                 reads=[rps[bg_]], writes=[rsg[s_]])
                            jl = jj + cc
                            P.op("dve", lambda e, s_=s_, bu_=bu_, n=n, jl=jl, c0=c0: e.scalar_tensor_tensor(
                                out=H[:, jl, c0:c0 + n], in0=sg[s_][:, 0:n], scalar=0.5, in1=ps[:, bu_, 0:n], op0=ALU.mult, op1=ALU.mult),
                                reads=[rsg[s_], rps[bu_]], writes=[rH[t]])
                    jj += nch
                for t in range(NT):
                    c0, n = tile_cols(g, t)
                    for m in range(KC):
                        b = bank()

                        def fn(e, b=b, m=m, c0=c0, n=n, half=half):
                            for j in range(FH):
                                ins = e.matmul(ps[:, b, 0:n], lhsT=WD[half][:, j, m * 128:(m + 1) * 128], rhs=H[:, j, c0:c0 + n],
                                               start=(j == 0), stop=(j == FH - 1))
                            return ins
                        P.op("pe", fn, reads=[rWD[half], rH[t]], writes=[rps[b]])
                        resid_evac(g, t, m, b, first=(half == 0), final=(half == NSPLIT - 1))
                        drain(1)
                    if half == NSPLIT - 1:
                        layer_norm(g, t, (l, 0 if which == 0 else 3))

        def mix(g, l):
            win = w_in[l]
            last_g = (g == NG - 1)
            if g == 0:
                P.op("dve", lambda e: e.memset(KK[:, :, 0:128], 0.0), writes=[res("KK")])
                P.op("dve", lambda e: e.memset(Vp[:, 0, :], 0.0), writes=[res("Vp")])
                P.op("dve", lambda e: e.memset(U[:, :, 0:2], 0.0), writes=[res("U")])
            else:
                P.op("dve", lambda e: e.tensor_copy(out=KK[:, :, 0:128], in_=KKc[l][:]), reads=[res("KKc%d" % l)], writes=[res("KK")])
                P.op("dve", lambda e: e.tensor_copy(out=Vp[:, 0, :], in_=Vpc[l][:]), reads=[res("Vpc%d" % l)], writes=[res("Vp")])
                P.op("dve", lambda e: e.tensor_copy(out=U[:, :, 0:2], in_=Uc[l][:]), reads=[res("Uc%d" % l)], writes=[res("U")])

            def wvx(e):
                outs = []
                for (dst, kv) in ((0, 0), (192, 0), (256, 1), (448, 1)):
                    outs.append(e.dma_start(out=WVX[:, :, dst:dst + 64],
                                            in_=win[:, 2176 + kv * 64:2176 + (kv + 1) * 64].rearrange("(k p) c -> p k c", p=128)))
                return outs
            P.op("pool", wvx, writes=[res("WVX")], dma="wvx", ndma=4)

            def mix_tile(t):
                c0, n = tile_cols(g, t)
                has_s = n > NPT
                w_hc, r_hc = getw([(0, 512, win[:, 1024:1536])])
                w_cg, r_cg = getw([(0, 512, win[:, 512:1024])])
                for c in range(4):
                    b1 = proj_chunk(w_hc, r_hc, c * 128, g, t)
                    P.op("act", lambda e, b1=b1: e.activation(out=hcs[:, 0:n], in_=ps[:, b1, 0:n], func=AF.Copy),
                         reads=[rps[b1]], writes=[res("hcs")])
                    b2 = proj_chunk(w_cg, r_cg, c * 128, g, t)
                    P.op("dve", lambda e, b2=b2, c=c: e.tensor_tensor(out=U[:, c, 2:2 + n], in0=ps[:, b2, 0:n], in1=hcs[:, 0:n], op=ALU.mult),
                         reads=[rps[b2], res("hcs")], writes=[res("U")])
                if g == 0 and t == 0:
                    P.op("dve", lambda e: e.tensor_scalar(out=U[:, :, 2 + HALO - 2:2 + HALO], in0=U[:, :, 2 + HALO - 2:2 + HALO],
                                                          scalar1=flg[:, 0:1], scalar2=None, op0=ALU.mult),
                         reads=[res("U"), res("flg")], writes=[res("U")])
                w_bg, r_bg = getw([(0, 512, win[:, 0:512])])
                for c in range(4):
                    b3 = proj_chunk(w_bg, r_bg, c * 128, g, t)
                    w0 = cw[:, (l * 3 + 0) * 4 + c:(l * 3 + 0) * 4 + c + 1]
                    w1 = cw[:, (l * 3 + 1) * 4 + c:(l * 3 + 1) * 4 + c + 1]
                    w2 = cw[:, (l * 3 + 2) * 4 + c:(l * 3 + 2) * 4 + c + 1]
                    P.op("dve", lambda e, c=c, w0=w0: e.tensor_scalar(out=acc[:, 0:NPT], in0=U[:, c, 0:NPT], scalar1=w0, scalar2=None, op0=ALU.mult),
                         reads=[res("U"), res("cw")], writes=[res("acc")])
                    P.op("dve", lambda e, c=c, w1=w1: e.scalar_tensor_tensor(out=acc[:, 0:NPT], in0=U[:, c, 1:1 + NPT], scalar=w1, in1=acc[:, 0:NPT],
                                                                             op0=ALU.mult, op1=ALU.add),
                         reads=[res("U"), res("acc")], writes=[res("acc")])
                    P.op("dve", lambda e, c=c, w2=w2: e.scalar_tensor_tensor(out=acc[:, 0:NPT], in0=U[:, c, 2:2 + NPT], scalar=w2, in1=acc[:, 0:NPT],
                                                                             op0=ALU.mult, op1=ALU.add),
                         reads=[res("U"), res("acc")], writes=[res("acc")])
                    if has_s:
                        so = ((l * 4 + c) * 2) * NS
                        P.op("dve", lambda e, so=so, w0=w0: e.tensor_scalar(out=acc[:, NPT:n], in0=stS[:, so:so + NS], scalar1=w0, scalar2=None, op0=ALU.mult),
                             reads=[res("stS"), res("acc")], writes=[res("acc")])
                        P.op("dve", lambda e, so=so, w1=w1: e.scalar_tensor_tensor(out=acc[:, NPT:n], in0=stS[:, so + NS:so + 2 * NS], scalar=w1, in1=acc[:, NPT:n],
                                                                                   op0=ALU.mult, op1=ALU.add),
                             reads=[res("stS"), res("acc")], writes=[res("acc")])
                        P.op("dve", lambda e, c=c, w2=w2: e.scalar_tensor_tensor(out=acc[:, NPT:n], in0=U[:, c, 2 + NPT:2 + n], scalar=w2, in1=acc[:, NPT:n],
                                                                                 op0=ALU.mult, op1=ALU.add),
                             reads=[res("U"), res("acc")], writes=[res("acc")])
                    P.op("dve", lambda e, c=c, b3=b3: e.tensor_tensor(out=Zb[:, c, 0:n], in0=acc[:, 0:n], in1=ps[:, b3, 0:n], op=ALU.mult),
                         reads=[res("acc"), rps[b3]], writes=[res("Zb")])
                if has_s:
                    def cs(e):
                        outs = []
                        for c in range(4):
                            so = ((l * 4 + c) * 2) * NS
                            outs.append(e.dma_start(out=convs[l, c, :, 0, :], in_=stS[:, so + NS:so + 2 * NS]))
                            outs.append(e.dma_start(out=convs[l, c, :, 1, :], in_=U[:, c, 2 + NPT:2 + n]))
                        return outs
                    P.op("sp", cs, reads=[res("U"), res("stS")], writes=[res("o_convs")], dma="o_convs", ndma=8)
                if last_g and t == NT - 1:
                    def cp(e):
                        outs = []
                        for c in range(4):
                            outs.append(e.dma_start(out=convp[l, c, :, :], in_=U[:, c, NPT:NPT + 2]))
                        return outs
                    P.op("sp", cp, reads=[res("U")], writes=[res("o_convp")], dma="o_convp", ndma=4)
                w_q, r_q = getw([(0, 512, win[:, 1536:2048])])
                for c in range(4):
                    b4 = proj_chunk(w_q, r_q, c * 128, g, t)
                    P.op("act", lambda e, b4=b4, c=c: e.activation(out=QT[:, c, 0:n], in_=ps[:, b4, 0:n], func=AF.Copy),
                         reads=[rps[b4]], writes=[res("QT")])
                w_k, r_k = getw([(0, 64, win[:, 2048:2112]), (64, 64, win[:, 2048:2112]),
                                 (128, 64, win[:, 2112:2176]), (192, 64, win[:, 2112:2176])])
                for kv in range(2):
                    b5 = proj_chunk(w_k, r_k, kv * 128, g, t)
                    P.op("act", lambda e, b5=b5, kv=kv: e.activation(out=KK[:, kv, 128:128 + n], in_=ps[:, b5, 0:n], func=AF.Copy),
                         reads=[rps[b5]], writes=[res("KK")])
                    if last_g and t == NT - 1:
                        s_ = 0
                        P.op("act", lambda e, b5=b5, kv=kv: e.activation(out=stg[0][kv * 64:(kv + 1) * 64, 0:128],
                                                                          in_=ps[kv * 64:(kv + 1) * 64, b5, NPT - 128:NPT], func=AF.Copy),
                             reads=[rps[b5]], writes=[rstg[0]])
                if last_g and t == NT - 1:
                    P.op("sp", lambda e: e.dma_start(out=wkp[l], in_=stg[0][:, 0:128]), reads=[rstg[0]], writes=[res("o_wkp")], dma="o_wkp")
                need(t)
                for blk in range(3):
                    bv = bank()

                    def fnv(e, bv=bv, blk=blk):
                        for k in range(KC):
                            ins = e.matmul(ps[:, bv, :], lhsT=Xb[:, k, c0 + blk * 128:c0 + (blk + 1) * 128], rhs=WVX[:, k, :],
                                           start=(k == 0), stop=(k == KC - 1))
                        return ins
                    P.op("pe", fnv, reads=[rXb[t], res("WVX")], writes=[rps[bv]])
                    P.op("act", lambda e, bv=bv, blk=blk: e.activation(out=Vp[:, 1 + blk, :], in_=ps[:, bv, :], func=AF.Copy),
                         reads=[rps[bv]], writes=[res("Vp")])
                    if last_g and t == NT - 1 and blk == 2:
                        P.op("act", lambda e, bv=bv: e.activation(out=stg[1][:, 0:64], in_=ps[:, bv, 0:64], func=AF.Copy), reads=[rps[bv]], writes=[rstg[1]])
                        P.op("act", lambda e, bv=bv: e.activation(out=stg[1][:, 64:128], in_=ps[:, bv, 256:320], func=AF.Copy), reads=[rps[bv]], writes=[rstg[1]])
                        P.op("sp", lambda e: e.dma_start(out=wvp[l], in_=stg[1][:, 0:128]), reads=[rstg[1]], writes=[res("o_wvp")], dma="o_wvp")
                if has_s:
                    sample_swa(g, t, l)
                mx = small[:, 0:8]
                nb = small[:, 8:16]
                sm = small[:, 16:24]
                es = small[:, 24:32]
                sk = sq[:, l * 8:(l + 1) * 8]
                nsk = nsq[:, l * 8:(l + 1) * 8]
                S4 = ps[:, 0:4, :].rearrange("p b (h c) -> p (b h) c", h=2)

                def st_S(blk):
                    qs = slice(blk * 128, (blk + 1) * 128)
                    ks = slice(blk * 128, blk * 128 + 256)
                    first = (g == 0 and t == 0 and blk == 2)
                    mk_ = maskFb if first else maskAb
                    mr_ = res("maskFb") if first else res("maskAb")

                    def fs(e):
                        for h in range(8):
                            hp = (h % 2) * 64
                            o = ps[:, h // 2, (h % 2) * 256:(h % 2) * 256 + 256]
                            e.matmul(o, lhsT=QT[hp:hp + 64, h // 2, qs], rhs=KK[hp:hp + 64, h // 4, ks], start=True, stop=False)
                            ins = e.matmul(o, lhsT=identb[:], rhs=mk_[:], start=False, stop=True)
                        return ins
                    P.op("pe", fs, reads=[res("QT"), res("KK"), res("identb"), mr_], writes=[rps[0], rps[1], rps[2], rps[3]])
                    state["bank"] = 0

                def st_A1(blk):
                    P.op("dve", lambda e: e.tensor_reduce(out=mx, in_=S4, axis=AX.X, op=ALU.max),
                         reads=[rps[0], rps[1], rps[2], rps[3]], writes=[res("small")])
                    P.op("dve", lambda e: e.scalar_tensor_tensor(out=nb, in0=mx, scalar=-0.125, in1=nsk, op0=ALU.mult, op1=ALU.min),
                         reads=[res("small"), res("nsq")], writes=[res("small")])
                    P.op("dve", lambda e: e.tensor_tensor(out=es, in0=sk, in1=nb, op=ALU.add), reads=[res("small"), res("sq")], writes=[res("small2")])

                def st_A2(blk):
                    def fe(e):
                        for h in range(8):
                            ins = e.activation(out=Pm[:, h, :], in_=ps[:, h // 2, (h % 2) * 256:(h % 2) * 256 + 256], func=AF.Exp,
                                               bias=nb[:, h:h + 1], scale=0.125, accum_out=sm[:, h:h + 1])
                        return ins
                    P.op("act", fe, reads=[rps[0], rps[1], rps[2], rps[3], res("small")], writes=[res("Pm"), res("small3")])
                    P.op("act", lambda e: e.activation(out=es, in_=es, func=AF.Exp), reads=[res("small2")], writes=[res("small2")])

                def st_B(blk):
                    P.op("dve", lambda e: e.tensor_tensor(out=sm, in0=sm, in1=es, op=ALU.add), reads=[res("small3"), res("small2")], writes=[res("small3")])
                    P.op("dve", lambda e: e.reciprocal(out=sm, in_=sm), reads=[res("small3")], writes=[res("small3")])

                    def fnorm(e):
                        for h in range(8):
                            ins = e.tensor_scalar(out=Pm[:, h, :], in0=Pm[:, h, :], scalar1=sm[:, h:h + 1], scalar2=None, op0=ALU.mult)
                        return ins
                    P.op("dve", fnorm, reads=[res("small3"), res("Pm")], writes=[res("Pm")])

                def st_T(blk):
                    qs = slice(blk * 128, (blk + 1) * 128)
                    for rr in range(2):
                        def ftr(e, rr=rr):
                            for hh in range(4):
                                h = rr * 4 + hh
                                for kb in range(2):
                                    ins = e.transpose(pst2[:, rr, (hh * 2 + kb) * 128:(hh * 2 + kb + 1) * 128], Pm[:, h, kb * 128:(kb + 1) * 128], identb[:])
                            return ins
                        P.op("pe", ftr, reads=[res("Pm"), res("identb")], writes=[rpstl[rr]])
                        P.op("act", lambda e, rr=rr: e.activation(out=PT[:, rr * 8:(rr + 1) * 8, :],
                                                                  in_=pst2[:, rr, :].rearrange("p (a q) -> p a q", q=128), func=AF.Copy),
                             reads=[rpstl[rr]], writes=[res("PT%d" % rr)])

                    def fpv(e):
                        for c in range(4):
                            kv = c // 2
                            i_ = 0
                            for hh in range(2):
                                h = 2 * c + hh
                                vcol = kv * 256 + hh * 128
                                for kb in range(2):
                                    ins = e.matmul(ps[:, 4, c * 128:(c + 1) * 128], lhsT=Vp[:, blk + kb, vcol:vcol + 128], rhs=PT[:, h * 2 + kb, :],
                                                   start=(i_ == 0), stop=(i_ == 3))
                                    i_ += 1
                        return ins
                    P.op("pe", fpv, reads=[res("Vp"), res("PT0"), res("PT1")], writes=[rps[4]])
                    P.op("act", lambda e: e.activation(out=Zb[:, 4:8, qs], in_=ps[:, 4, :].rearrange("p (c q) -> p c q", q=128), func=AF.Copy),
                         reads=[rps[4]], writes=[res("Zb")])

                blk0 = {0: 1, 1: 2}[l] if (g == 0 and t == 0) else 0
                st_S(blk0)
                st_A1(blk0)
                st_A2(blk0)
                for blk in range(blk0, 3):
                    if blk + 1 < 3:
                        st_S(blk + 1)
                    st_B(blk)
                    if blk + 1 < 3:
                        st_A1(blk + 1)
                    st_T(blk)
                    if blk + 1 < 3:
                        st_A2(blk + 1)
                w_o = [getw([(0, 512, w_out[l][:, 0:512])]), getw([(0, 512, w_out[l][:, 512:1024])])]
                for m in range(KC):
                    wt, wr = w_o[m // 4]
                    b = proj_chunk(wt, wr, (m % 4) * 128, g, t, rhs_src=Zb, rhs_res=res("Zb"))
                    resid_evac(g, t, m, b)
                layer_norm(g, t, (l, 1))
                P.op("dve", lambda e: e.tensor_copy(out=KK[:, :, 0:128], in_=KK[:, :, NPT:NPT + 128]), reads=[res("KK")], writes=[res("KK")])
                P.op("dve", lambda e: e.tensor_copy(out=Vp[:, 0, :], in_=Vp[:, 3, :]), reads=[res("Vp")], writes=[res("Vp")])
                P.op("dve", lambda e: e.tensor_copy(out=U[:, :, 0:2], in_=U[:, :, NPT:NPT + 2]), reads=[res("U")], writes=[res("U")])
            for t in range(NT):
                mix_tile(t)
            if g == 0:
                P.op("dve", lambda e: e.tensor_copy(out=KKc[l][:], in_=KK[:, :, 0:128]), reads=[res("KK")], writes=[res("KKc%d" % l)])
                P.op("dve", lambda e: e.tensor_copy(out=Vpc[l][:], in_=Vp[:, 0, :]), reads=[res("Vp")], writes=[res("Vpc%d" % l)])
                P.op("dve", lambda e: e.tensor_copy(out=Uc[l][:], in_=U[:, :, 0:2]), reads=[res("U")], writes=[res("Uc%d" % l)])

        def tok_major(g, t, wparts_list, ncol_list, dst_col0):
            need(t)
            c0, n = tile_cols(g, t)
            col = dst_col0
            for parts, ncol in zip(wparts_list, ncol_list):
                wt, wr = getw(parts)
                b = bank()

                def fn(e, wt=wt, b=b, ncol=ncol):
                    for k in range(KC):
                        ins = e.matmul(ps[0:NS, b, 0:ncol], lhsT=Xb[:, k, c0 + NPT:c0 + n], rhs=wt[:, k, 0:ncol],
                                       start=(k == 0), stop=(k == KC - 1))
                    return ins
                P.op("pe", fn, reads=[wr, rXb[t]], writes=[rps[b]])
                P.op("act", lambda e, b=b, ncol=ncol, col=col: e.activation(out=tokS[:, col:col + ncol], in_=ps[0:NS, b, 0:ncol], func=AF.Copy),
                     reads=[rps[b]], writes=[res("tokS")])
                col += ncol

        def to_feature_major(nchunks, src_col0, zc0, zcols):
            def ftr(e):
                for c in range(nchunks):
                    ins = e.transpose(ps[:, 4, c * NS:(c + 1) * NS], tokS[:, src_col0 + c * 128:src_col0 + (c + 1) * 128], identf[0:NS, 0:NS])
                return ins
            P.op("pe", ftr, reads=[res("tokS"), res("identf")], writes=[rps[4]])
            P.op("act", lambda e: e.activation(out=Zb[:, zc0:zc0 + nchunks, zcols],
                                               in_=ps[:, 4, 0:nchunks * NS].rearrange("p (c s) -> p c s", s=NS), func=AF.Copy),
                 reads=[rps[4]], writes=[res("Zb")])

        def sample_swa(g, t, l):
            win = w_in[l]
            c0, n = tile_cols(g, t)
            krep = [(h * 64, 64, win[:, 2048 + (h // 4) * 64:2048 + (h // 4 + 1) * 64]) for h in range(8)]
            vrep = [(h * 64, 64, win[:, 2176 + (h // 4) * 64:2176 + (h // 4 + 1) * 64]) for h in range(8)]
            tok_major(g, t, [[(0, 512, win[:, 1536:2048])], krep, vrep], [512, 512, 512], 0)

            def f1(e):
                return [e.dma_start(out=scr_q, in_=tokS[:, 0:512]),
                        e.dma_start(out=scr_k, in_=tokS[:, 512:1024]),
                        e.dma_start(out=scr_v, in_=tokS[:, 1024:1536]),
                        e.dma_start(out=wks[l][:, 127, :].rearrange("b (kv d) -> b kv d", d=64),
                                    in_=tokS[:, 512:1024].rearrange("b (kv x) -> b kv x", kv=2)[:, :, 0:64]),
                        e.dma_start(out=wvs[l][:, 127, :].rearrange("b (kv d) -> b kv d", d=64),
                                    in_=tokS[:, 1024:1536].rearrange("b (kv x) -> b kv x", kv=2)[:, :, 0:64])]
            P.op("sp", f1, reads=[res("tokS")], writes=[res("scr_qkv")], dma="scr1", ndma=5)

            def f2(e):
                return [e.dma_start(out=qx[:, 0:64], in_=scr_q.rearrange("b (h d) -> (b h) d", d=64)),
                        e.dma_start(out=KX[:, 0:128, :], in_=wk_s[l].rearrange("p (c d) -> p c d", d=64)),
                        e.dma_start(out=KX[:, 128, :], in_=scr_k.rearrange("b (h d) -> (b h) d", d=64))]
            P.op("sp", f2, reads=[res("scr_qkv")], writes=[res("qx"), rSB[0], rSB[1]], dma="scr2", ndma=3, arena=True)
            P.op("dve", lambda e: e.tensor_tensor(out=KX[:, :, :], in0=KX[:, :, :], in1=qx[:, 0:64].unsqueeze(1).to_broadcast([128, 129, 64]), op=ALU.mult),
                 reads=[res("qx"), rSB[0], rSB[1]], writes=[rSB[0], rSB[1]])
            sc = ssm[:, 0:129]
            P.op("dve", lambda e: e.tensor_reduce(out=sc, in_=KX[:, :, :], axis=AX.X, op=ALU.add), reads=[rSB[0], rSB[1]], writes=[res("ssm")])
            mx = small[:, 32:33]
            nb = small[:, 33:34]
            sm = small[:, 34:35]
            es = small[:, 35:36]
            sk = ss[:, l:l + 1]
            P.op("dve", lambda e: e.tensor_reduce(out=mx, in_=sc, axis=AX.X, op=ALU.max), reads=[res("ssm")], writes=[res("smallS")])
            P.op("dve", lambda e: e.scalar_tensor_tensor(out=mx, in0=mx, scalar=0.125, in1=sk, op0=ALU.mult, op1=ALU.max),
                 reads=[res("smallS"), res("ss")], writes=[res("smallS")])
            P.op("dve", lambda e: e.tensor_scalar(out=nb, in0=mx, scalar1=-1.0, scalar2=None, op0=ALU.mult), reads=[res("smallS")], writes=[res("smallS")])
            P.op("dve", lambda e: e.tensor_tensor(out=es, in0=sk, in1=nb, op=ALU.add), reads=[res("smallS"), res("ss")], writes=[res("smallS")])
            P.op("act", lambda e: e.activation(out=sc, in_=sc, func=AF.Exp, bias=nb, scale=0.125), reads=[res("ssm"), res("smallS")], writes=[res("ssm")])
            P.op("act", lambda e: e.activation(out=es, in_=es, func=AF.Exp), reads=[res("smallS")], writes=[res("smallS")])
            P.op("dve", lambda e: e.tensor_reduce(out=sm, in_=sc, axis=AX.X, op=ALU.add), reads=[res("ssm")], writes=[res("smallS")])
            P.op("dve", lambda e: e.tensor_tensor(out=sm, in0=sm, in1=es, op=ALU.add), reads=[res("smallS")], writes=[res("smallS")])
            P.op("dve", lambda e: e.reciprocal(out=sm, in_=sm), reads=[res("smallS")], writes=[res("smallS")])
            P.op("dve", lambda e: e.tensor_scalar(out=sc, in0=sc, scalar1=sm, scalar2=None, op0=ALU.mult), reads=[res("ssm"), res("smallS")], writes=[res("ssm")])

            def f3(e):
                return [e.dma_start(out=KX[:, 0:128, :], in_=wv_s[l].rearrange("p (c d) -> p c d", d=64)),
                        e.dma_start(out=KX[:, 128, :], in_=scr_v.rearrange("b (h d) -> (b h) d", d=64))]
            P.op("sp", f3, reads=[res("scr_qkv")], writes=[rSB[0], rSB[1]], dma="scr3", ndma=2, arena=True)
            P.op("dve", lambda e: e.tensor_tensor(out=KX[:, :, :], in0=KX[:, :, :], in1=sc.unsqueeze(2).to_broadcast([128, 129, 64]), op=ALU.mult),
                 reads=[res("ssm"), rSB[0], rSB[1]], writes=[rSB[0], rSB[1]])
            P.op("dve", lambda e: e.tensor_reduce(out=oacc[:, 0:64], in_=KX[:, :, :].rearrange("p c d -> p d c"), axis=AX.X, op=ALU.add),
                 reads=[rSB[0], rSB[1]], writes=[res("oacc")])
            P.op("sp", lambda e: e.dma_start(out=scr_o.rearrange("b (h d) -> (b h) d", d=64), in_=oacc[:, 0:64]),
                 reads=[res("oacc")], writes=[res("scr_o")], dma="scr4")
            P.op("sp", lambda e: e.dma_start(out=tokS[:, 1536:2048], in_=scr_o), reads=[res("scr_o")], writes=[res("tokS")], dma="scr5", arena=True)
            to_feature_major(4, 1536, 4, slice(NPT, n))

        def cross(g, l):
            if g > 0:
                P.op("sp", lambda e: [e.dma_start(out=MKT[:].rearrange("p c t -> p (c t)"), in_=scr_mkt[l]),
                                      e.dma_start(out=MV[:].rearrange("p c t -> p (c t)"), in_=scr_mv[l])],
                     reads=[res("scr_mk%d" % l)], writes=[res("MKT"), res("MV")], dma="mkin", ndma=2)
            for m in range(KC if (KCROSS >= 1 and g == 0) else 0):
                if m % 4 == 0:
                    memTb, rmem = getw([(0, 256, memT)])
                    wt, wr = getw([(0, 512, w_mk[l][:, (m // 4) * 512:(m // 4 + 1) * 512])])
                b = bank()

                def fn(e, wt=wt, b=b, m=m, memTb=memTb):
                    for k in range(KC):
                        ins = e.matmul(ps[:, b, 0:256], lhsT=wt[:, k, (m % 4) * 128:(m % 4 + 1) * 128], rhs=memTb[:, k, 0:256],
                                       start=(k == 0), stop=(k == KC - 1))
                    return ins
                P.op("pe", fn, reads=[wr, rmem], writes=[rps[b]])
                P.op("act", lambda e, b=b, m=m: e.activation(out=MKT[:, m, :], in_=ps[:, b, 0:256], func=AF.Copy), reads=[rps[b]], writes=[res("MKT")])
                if g == 0 and 'NOMKP' not in os.environ:
                    s_ = nxt("stg")
                    P.op("act", lambda e, b=b, s_=s_: e.activation(out=stg[s_][:, 0:256], in_=ps[:, b, 0:256], func=AF.Copy), reads=[rps[b]], writes=[rstg[s_]])
                    P.op("sp", lambda e, m=m, s_=s_: e.dma_start(out=mkp[l, m * 128:(m + 1) * 128, :], in_=stg[s_][:, 0:256]),
                         reads=[rstg[s_]], writes=[res("o_mkp")], dma="o_mk%d" % s_)
            for hf in range(2 if (KCROSS >= 2 and g == 0) else 0):
                memTb, rmem = getw([(0, 256, memT)])
                wt, wr = getw([(0, 512, w_mv[l][:, hf * 512:(hf + 1) * 512])])
                for mc in range(2):
                    b = bank()

                    def fn(e, wt=wt, b=b, mc=mc, memTb=memTb):
                        for k in range(KC):
                            ins = e.matmul(ps[:, b, :], lhsT=memTb[:, k, mc * 128:(mc + 1) * 128], rhs=wt[:, k, :],
                                           start=(k == 0), stop=(k == KC - 1))
                        return ins
                    P.op("pe", fn, reads=[wr, rmem], writes=[rps[b]])
                    P.op("act", lambda e, b=b, mc=mc, hf=hf: e.activation(out=MV[:, mc, hf * 512:(hf + 1) * 512], in_=ps[:, b, :], func=AF.Copy),
                         reads=[rps[b]], writes=[res("MV")])
                    if g == 0:
                        s_ = nxt("stg")
                        P.op("act", lambda e, b=b, s_=s_: e.activation(out=stg[s_][:, :], in_=ps[:, b, :], func=AF.Copy), reads=[rps[b]], writes=[rstg[s_]])
                        P.op("sp", lambda e, mc=mc, hf=hf, s_=s_: e.dma_start(out=mvp[l, mc * 128:(mc + 1) * 128, hf * 512:(hf + 1) * 512], in_=stg[s_][:, :]),
                             reads=[rstg[s_]], writes=[res("o_mvp")], dma="o_mk%d" % s_)
            if g == 0:
                P.op("sp", lambda e: [e.dma_start(out=scr_mkt[l], in_=MKT[:].rearrange("p c t -> p (c t)")),
                                      e.dma_start(out=scr_mv[l], in_=MV[:].rearrange("p c t -> p (c t)"))],
                     reads=[res("MKT"), res("MV")], writes=[res("scr_mk%d" % l)], dma="mkout", ndma=2)

            def cross_tile(t):
                c0, n = tile_cols(g, t)
                has_s = n > NPT
                wq = [getw([(0, 512, w_cq[l][:, 0:512])]), getw([(0, 512, w_cq[l][:, 512:1024])])]
                for m in range(KC):
                    wt, wr = wq[m // 4]
                    b = proj_chunk(wt, wr, (m % 4) * 128, g, t)
                    P.op("act", lambda e, b=b, m=m: e.activation(out=QcT[:, m, 0:n], in_=ps[:, b, 0:n], func=AF.Copy), reads=[rps[b]], writes=[res("QcT")])
                if has_s:
                    sample_cross(g, t, l)
                mx = small[:, 40:44]
                nb = small[:, 44:48]
                sm = small[:, 48:52]
                S2 = ps[:, 0:2, :].rearrange("p b (h c) -> p (b h) c", h=2)

                def ct_S(blk):
                    qs = slice(blk * 128, (blk + 1) * 128)

                    def fs(e):
                        for h in range(4):
                            o = ps[:, h // 2, (h % 2) * 256:(h % 2) * 256 + 256]
                            e.matmul(o, lhsT=QcT[:, 2 * h, qs], rhs=MKT[:, 2 * h, :], start=True, stop=False)
                            ins = e.matmul(o, lhsT=QcT[:, 2 * h + 1, qs], rhs=MKT[:, 2 * h + 1, :], start=False, stop=True)
                        return ins
                    P.op("pe", fs, reads=[res("QcT"), res("MKT")], writes=[rps[0], rps[1]])
                    state["bank"] = 2

                def ct_A1(blk):
                    P.op("dve", lambda e: e.tensor_reduce(out=mx, in_=S2, axis=AX.X, op=ALU.max), reads=[rps[0], rps[1]], writes=[res("smallC")])
                    P.op("dve", lambda e: e.tensor_scalar(out=nb, in0=mx, scalar1=-1.0 / 16.0, scalar2=None, op0=ALU.mult), reads=[res("smallC")], writes=[res("smallC")])

                def ct_A2(blk):
                    def fe(e):
                        for h in range(4):
                            ins = e.activation(out=Pm[:, h, :], in_=ps[:, h // 2, (h % 2) * 256:(h % 2) * 256 + 256], func=AF.Exp,
                                               bias=nb[:, h:h + 1], scale=1.0 / 16.0, accum_out=sm[:, h:h + 1])
                        return ins
                    P.op("act", fe, reads=[rps[0], rps[1], res("smallC")], writes=[res("Pm"), res("smallC2")])

                def ct_B(blk):
                    P.op("dve", lambda e: e.reciprocal(out=sm, in_=sm), reads=[res("smallC2")], writes=[res("smallC2")])

                    def fnorm(e):
                        for h in range(4):
                            ins = e.tensor_scalar(out=Pm[:, h, :], in0=Pm[:, h, :], scalar1=sm[:, h:h + 1], scalar2=None, op0=ALU.mult)
                        return ins
                    P.op("dve", fnorm, reads=[res("smallC2"), res("Pm")], writes=[res("Pm")])

                def ct_T(blk):
                    qs = slice(blk * 128, (blk + 1) * 128)

                    def ftr(e):
                        for h in range(4):
                            for mc in range(2):
                                ins = e.transpose(pst2[:, blk % 2, (h * 2 + mc) * 128:(h * 2 + mc + 1) * 128], Pm[:, h, mc * 128:(mc + 1) * 128], identb[:])
                        return ins
                    P.op("pe", ftr, reads=[res("Pm"), res("identb")], writes=[rpstl[blk % 2]])
                    P.op("act", lambda e: e.activation(out=PT[:, 0:8, :], in_=pst2[:, blk % 2, :].rearrange("p (a q) -> p a q", q=128), func=AF.Copy),
                         reads=[rpstl[blk % 2]], writes=[res("PT0")])
                    for hf in range(2):
                        def fpv(e, hf=hf):
                            for cc in range(4):
                                c = hf * 4 + cc
                                h = c // 2
                                for mc in range(2):
                                    ins = e.matmul(ps[:, 4 + hf, cc * 128:(cc + 1) * 128], lhsT=MV[:, mc, c * 128:(c + 1) * 128], rhs=PT[:, h * 2 + mc, :],
                                                   start=(mc == 0), stop=(mc == 1))
                            return ins
                        P.op("pe", fpv, reads=[res("MV"), res("PT0")], writes=[rps[4 + hf]])
                        P.op("act", lambda e, hf=hf: e.activation(out=Zb[:, hf * 4:hf * 4 + 4, qs],
                                                                  in_=ps[:, 4 + hf, :].rearrange("p (c q) -> p c q", q=128), func=AF.Copy),
                             reads=[rps[4 + hf]], writes=[res("Zb")])

                nblk = (n - (NS if has_s else 0)) // 128
                if KCROSS >= 4:
                    ct_S(0)
                    ct_A1(0)
                    ct_A2(0)
                    for blk in range(nblk):
                        if blk + 1 < nblk:
                            ct_S(blk + 1)
                        ct_B(blk)
                        if blk + 1 < nblk:
                            ct_A1(blk + 1)
                        ct_T(blk)
                        if blk + 1 < nblk:
                            ct_A2(blk + 1)
                if KCROSS < 5:
                    return
                w_o = [getw([(0, 512, w_co[l][:, 0:512])]), getw([(0, 512, w_co[l][:, 512:1024])])]
                for m in range(KC):
                    wt, wr = w_o[m // 4]
                    b = proj_chunk(wt, wr, (m % 4) * 128, g, t, rhs_src=Zb, rhs_res=res("Zb"))
                    resid_evac(g, t, m, b)
                layer_norm(g, t, (l, 2))
            state["trim"] = {0: 128, 1: 256}[l] if g == 0 else 0
            try:
                for t in range(NT if KCROSS >= 3 else 0):
                    cross_tile(t)
            finally:
                state["trim"] = 0

        def sample_cross(g, t, l):
            c0, n = tile_cols(g, t)
            tok_major(g, t, [[(0, 512, w_cq[l][:, 0:512])], [(0, 512, w_cq[l][:, 512:1024])]], [512, 512], 0)

            def f1(e):
                src = tokS[:, 0:1024].rearrange("b (h d) -> b h d", d=256)
                return [e.dma_start(out=scr_qc[:, :, 0, :], in_=src), e.dma_start(out=scr_qc[:, :, 1, :], in_=src)]
            P.op("sp", f1, reads=[res("tokS")], writes=[res("scr_qc")], dma="scc1", ndma=2)
            P.op("sp", lambda e: e.dma_start(out=qx[:, :], in_=scr_qc.rearrange("b h m d -> (b h m) d")),
                 reads=[res("scr_qc")], writes=[res("qx")], dma="scc2")
            sc = ssm[:, 0:128]
            for i in range(8):
                s_ = i % 2
                P.op("sp", lambda e, i=i, s_=s_: e.dma_start(out=SBh[s_][:, :, :], in_=cmk[l][:, i * 4096:(i + 1) * 4096].rearrange("p (m d) -> p m d", d=256)),
                     writes=[rSB[s_]], dma="sbh%d" % s_, arena=True)
                P.op("dve", lambda e, s_=s_: e.tensor_tensor(out=SBh[s_][:, :, :], in0=SBh[s_][:, :, :],
                                                             in1=qx[:, :].unsqueeze(1).to_broadcast([128, 16, 256]), op=ALU.mult),
                     reads=[rSB[s_], res("qx")], writes=[rSB[s_]])
                P.op("dve", lambda e, s_=s_, i=i: e.tensor_reduce(out=sc[:, i * 16:(i + 1) * 16], in_=SBh[s_][:, :, :], axis=AX.X, op=ALU.add),
                     reads=[rSB[s_]], writes=[res("ssm")])
            P.op("sp", lambda e: e.dma_start(out=scr_sc, in_=sc), reads=[res("ssm")], writes=[res("scr_sc")], dma="scc3")
            s2 = ssm[0:64, 0:256]
            P.op("sp", lambda e: e.dma_start(out=s2, in_=scr_sc.rearrange("(a mh) m -> a (mh m)", mh=2)), reads=[res("scr_sc")], writes=[res("ssm")], dma="scc4")
            mx = small[0:64, 56:57]
            nb = small[0:64, 57:58]
            sm = small[0:64, 58:59]
            P.op("dve", lambda e: e.tensor_reduce(out=mx, in_=s2, axis=AX.X, op=ALU.max), reads=[res("ssm")], writes=[res("smallX")])
            P.op("dve", lambda e: e.tensor_scalar(out=nb, in0=mx, scalar1=-1.0 / 16.0, scalar2=None, op0=ALU.mult), reads=[res("smallX")], writes=[res("smallX")])
            P.op("act", lambda e: e.activation(out=s2, in_=s2, func=AF.Exp, bias=nb, scale=1.0 / 16.0), reads=[res("ssm"), res("smallX")], writes=[res("ssm")])
            P.op("dve", lambda e: e.tensor_reduce(out=sm, in_=s2, axis=AX.X, op=ALU.add), reads=[res("ssm")], writes=[res("smallX")])
            P.op("dve", lambda e: e.reciprocal(out=sm, in_=sm), reads=[res("smallX")], writes=[res("smallX")])
            P.op("dve", lambda e: e.tensor_scalar(out=s2, in0=s2, scalar1=sm, scalar2=None, op0=ALU.mult), reads=[res("ssm"), res("smallX")], writes=[res("ssm")])
            P.op("sp", lambda e: e.dma_start(out=scr_pc.rearrange("(a mh) m -> a (mh m)", mh=2), in_=s2), reads=[res("ssm")], writes=[res("scr_pc")], dma="scc5")
            pc = ssm[:, 128:256]
            P.op("sp", lambda e: e.dma_start(out=pc, in_=scr_pc), reads=[res("scr_pc")], writes=[res("ssm")], dma="scc6")
            for i in range(8):
                s_ = i % 2
                P.op("sp", lambda e, i=i, s_=s_: e.dma_start(out=SBh[s_][:, :, :], in_=cmv[l][:, i * 4096:(i + 1) * 4096].rearrange("p (m d) -> p m d", d=256)),
                     writes=[rSB[s_]], dma="sbh%d" % s_, arena=True)
                P.op("dve", lambda e, s_=s_, i=i: e.tensor_tensor(out=SBh[s_][:, :, :], in0=SBh[s_][:, :, :],
                                                                  in1=pc[:, i * 16:(i + 1) * 16].unsqueeze(2).to_broadcast([128, 16, 256]), op=ALU.mult),
                     reads=[rSB[s_], res("ssm")], writes=[rSB[s_]])
                if i == 0:
                    P.op("dve", lambda e, s_=s_: e.tensor_reduce(out=oacc[:, :], in_=SBh[s_][:, :, :].rearrange("p m d -> p d m"), axis=AX.X, op=ALU.add),
                         reads=[rSB[s_]], writes=[res("oacc")])
                else:
                    P.op("dve", lambda e, s_=s_: e.tensor_reduce(out=qx[:, :], in_=SBh[s_][:, :, :].rearrange("p m d -> p d m"), axis=AX.X, op=ALU.add),
                         reads=[rSB[s_]], writes=[res("qx")])
                    P.op("dve", lambda e: e.tensor_tensor(out=oacc[:, :], in0=oacc[:, :], in1=qx[:, :], op=ALU.add),
                         reads=[res("qx"), res("oacc")], writes=[res("oacc")])
            P.op("sp", lambda e: e.dma_start(out=scr_oc, in_=oacc[:, :]), reads=[res("oacc")], writes=[res("scr_oc")], dma="scc7")
            P.op("sp", lambda e: e.dma_start(out=tokS[:, 0:2048], in_=scr_oc.rearrange("(b x) d -> b (x d)", b=NS)),
                 reads=[res("scr_oc")], writes=[res("tokS")], dma="scc8", arena=True)
            tv = tokS[:, 0:2048].rearrange("b (h m d) -> b h m d", h=4, m=2)
            P.op("dve", lambda e: e.tensor_tensor(out=tv[:, :, 0, :], in0=tv[:, :, 0, :], in1=tv[:, :, 1, :], op=ALU.add),
                 reads=[res("tokS")], writes=[res("tokS")])
            def ftr(e):
                for c in range(8):
                    h, dc = c // 2, c % 2
                    ins = e.transpose(ps[:, 4, c * NS:(c + 1) * NS], tv[:, h, 0, dc * 128:(dc + 1) * 128], identf[0:NS, 0:NS])
                return ins
            P.op("pe", ftr, reads=[res("tokS"), res("identf")], writes=[rps[4]])
            P.op("act", lambda e: e.activation(out=Zb[:, 0:8, NPT:n], in_=ps[:, 4, 0:8 * NS].rearrange("p (c s) -> p c s", s=NS), func=AF.Copy),
                 reads=[rps[4]], writes=[res("Zb")])

        nstep = 0
        for g in range(NG):
            gc0 = g * GP
            for t in range(NT):
                c0, n = tile_cols(g, t)

                def fx(e, c0=c0, n=n, gc0=gc0):
                    return e.dma_start(out=X[:, :, c0:c0 + n], in_=xT[:, gc0 + c0:gc0 + c0 + n].rearrange("(k p) c -> p k c", p=128))
                P.op("sp", fx, writes=rX[t] + [], dma="xin%d" % t)
                P.op("dve", lambda e, c0=c0, n=n: e.tensor_copy(out=Xb[:, :, c0:c0 + n], in_=X[:, :, c0:c0 + n]), reads=rX[t], writes=[rXb[t]])
            for l in range(L):
                for si, stage in enumerate((lambda: ffn(g, l, 0), lambda: mix(g, l), lambda: cross(g, l), lambda: ffn(g, l, 1))):
                    if nstep < KSTOP:
                        if si > 0:
                            P.barrier()
                        stage()
                    nstep += 1
            drain(10 ** 6)
            for t in range(NT):
                c0, n = tile_cols(g, t)
                gl = gc0 + c0
                lo = max(gl, HALO)
                hi = gl + NPT
                if hi > lo:
                    def fo(e, lo=lo, hi=hi, gl=gl, c0=c0):
                        return e.dma_start(out=yT[:, lo - HALO:hi - HALO].rearrange("(k p) c -> p k c", p=128),
                                           in_=X[:, :, c0 + (lo - gl):c0 + (hi - gl)])
                    P.op("sp", fo, reads=rX[t], writes=[res("o_y")], dma="yout")
                if n > NPT:
                    def fo2(e, c0=c0, n=n):
                        return e.dma_start(out=yT[:, OWN:OWN + NS].rearrange("(k p) c -> p k c", p=128), in_=X[:, :, c0 + NPT:c0 + n])
                    P.op("sp", fo2, reads=rX[t], writes=[res("o_y")], dma="yout")
            P.barrier()

    run_build(Prog(), None)
    reqs = list(state["rec"])
    P = Prog()
    run_build(P, reqs)
    P.emit(nc)
    if 'KVERB' in os.environ:
        print('SEMCOUNTS', {k: v for k, v in P.cnt.items()})
    return nc


_NC = None


def _prep(inputs):
    f = lambda a: np.ascontiguousarray(np.asarray(a), dtype=np.float32)
    xp = f(inputs["x_prompt"]); xs = f(inputs["x_sample"]); mem = f(inputs["mem_prompt"])
    cwk = f(inputs["cache_win_k"]); cwv = f(inputs["cache_win_v"]); stc = f(inputs["state_conv"])
    cmk = f(inputs["cache_mem_k"]); cmv = f(inputs["cache_mem_v"])
    ln_g = f(inputs["ln_g"]); ln_b = f(inputs["ln_b"])
    shared = {
        "w_gu1": f(inputs["ffn1_w_gu"]), "w_gu2": f(inputs["ffn2_w_gu"]),
        "w_dn1": f(inputs["ffn1_w_down"]), "w_dn2": f(inputs["ffn2_w_down"]),
        "w_in": f(inputs["w_in"]), "w_out": f(inputs["w_out"]), "w_cq": f(inputs["w_cq"]),
        "w_mk": f(inputs["w_mk"]), "w_mv": f(inputs["w_mv"]), "w_co": f(inputs["w_co"]),
    }
    gvec = np.zeros((128, L * 4 * 8 * 2), np.float32)
    for l in range(L):
        for i in range(4):
            for k in range(8):
                gvec[:, gcol(l, i, k)] = ln_g[l, i, k * 128:(k + 1) * 128]
                gvec[:, gcol(l, i, k) + 1] = ln_b[l, i, k * 128:(k + 1) * 128]
    cwt = f(inputs["conv_w"])
    convw = np.zeros((128, L * 3 * 4), np.float32)
    for l in range(L):
        for tap in range(3):
            for c in range(4):
                convw[:, (l * 3 + tap) * 4 + c] = cwt[l, tap, c * 128:(c + 1) * 128]
    snk = f(inputs["attn_sinks"])
    sinkq = np.ascontiguousarray(np.broadcast_to(snk.reshape(1, L * 8), (128, L * 8)))
    sinks = np.ascontiguousarray(np.stack([snk[l][np.arange(128) % 8] for l in range(L)], axis=1))
    ident = np.eye(128, dtype=np.float32)
    a = np.arange(128)[:, None]; c = np.arange(256)[None, :]
    maskA = np.where((c >= a) & (c <= a + 128), 0.0, NEG).astype(np.float32)
    maskF0 = maskA.copy(); maskF0[:, :128] = NEG
    shared.update(gvec=gvec, convw=convw, sinkq=sinkq, sinks=sinks, ident=ident, maskA=maskA)
    in_maps = []
    for core in range(8):
        b, ch = core // 4, core % 4
        start = ch * OWN
        seg = np.zeros((HALO + OWN, D), np.float32)
        lo = start - HALO
        if lo < 0:
            seg[HALO:] = xp[b, 0:OWN]
        else:
            seg[:] = xp[b, lo:start + OWN]
        s0 = core * NS
        xT = np.ascontiguousarray(np.concatenate([seg, xs[s0:s0 + NS, 0, :]], axis=0).T)
        m = dict(shared)
        m["xT"] = xT
        m["memT"] = np.ascontiguousarray(mem[b].T)
        m["maskF"] = maskF0 if ch == 0 else maskA
        m["flag"] = np.full((128, 1), 0.0 if ch == 0 else 1.0, np.float32)
        st = stc[:, s0:s0 + NS]
        st = st.reshape(L, NS, 2, 4, 128).transpose(4, 0, 3, 2, 1)
        m["stT"] = np.ascontiguousarray(st.reshape(128, L * 4 * 2 * NS))
        m["wk_o"] = np.ascontiguousarray(cwk[:, s0:s0 + NS].reshape(L, NS, 128, 128))
        m["wv_o"] = np.ascontiguousarray(cwv[:, s0:s0 + NS].reshape(L, NS, 128, 128))
        for nm, arr in (("wk_s", cwk), ("wv_s", cwv)):
            w_ = arr[:, s0:s0 + NS]
            w_ = w_.transpose(0, 1, 3, 2, 4)
            w_ = np.repeat(w_[:, :, :, None], 4, axis=3)
            m[nm] = np.ascontiguousarray(w_.reshape(L, 128, 8192))
        for nm, arr in (("cmk", cmk), ("cmv", cmv)):
            w_ = arr[:, s0:s0 + NS]
            w_ = w_.reshape(L, NS, 2, 128, 4, 256).transpose(0, 1, 4, 2, 3, 5)
            m[nm] = np.ascontiguousarray(w_.reshape(L, 128, 32768))
        in_maps.append(m)
    return in_maps


def kernel(**inputs):
    global _NC
    if _NC is None:
        _NC = build_nc()
    in_maps = _prep(inputs)
    res = run_bass_kernel_spmd(_NC, in_maps, core_ids=list(range(8)))
    R_ = res.results
    B, S = 2, 8192
    yp = np.zeros((B, S, D), np.float32)
    ys = np.zeros((128, 1, D), np.float32)
    wkp = np.zeros((L, B, 128, 2, 64), np.float32); wvp = np.zeros_like(wkp)
    cvp = np.zeros((L, B, 2, 512), np.float32)
    mkp = np.zeros((L, B, 256, 4, 256), np.float32); mvp = np.zeros_like(mkp)
    wks = np.zeros((L, 128, 128, 2, 64), np.float32); wvs = np.zeros_like(wks)
    cvs = np.zeros((L, 128, 2, 512), np.float32)
    for core in range(8):
        r = R_[core]
        b, ch = core // 4, core % 4
        s0 = core * NS
        yT = np.asarray(r["yT"])
        yp[b, ch * OWN:(ch + 1) * OWN] = yT[:, 0:OWN].T
        ys[s0:s0 + NS, 0] = yT[:, OWN:OWN + NS].T
        wks[:, s0:s0 + NS] = np.asarray(r["wks"]).reshape(L, NS, 128, 2, 64)
        wvs[:, s0:s0 + NS] = np.asarray(r["wvs"]).reshape(L, NS, 128, 2, 64)
        cs = np.asarray(r["convs"])
        cvs[:, s0:s0 + NS] = cs.transpose(0, 4, 3, 1, 2).reshape(L, NS, 2, 512)
        if ch == 3:
            wkp[:, b] = np.asarray(r["wkp"]).transpose(0, 2, 1).reshape(L, 128, 2, 64)
            wvp[:, b] = np.asarray(r["wvp"]).reshape(L, 128, 2, 64)
            cp = np.asarray(r["convp"])
            cvp[:, b] = cp.transpose(0, 3, 1, 2).reshape(L, 2, 512)
        if ch == 0:
            mkp[:, b] = np.asarray(r["mkp"]).transpose(0, 2, 1).reshape(L, 256, 4, 256)
            mvp[:, b] = np.asarray(r["mvp"]).reshape(L, 256, 4, 256)
    return (yp, ys, wkp, wvp, cvp, mkp, mvp, wks, wvs, cvs)
```

```python
import contextlib
import numpy as np
import concourse.bass as bass
import concourse.mybir as mybir
from concourse.bass_utils import run_bass_kernel_spmd

F32 = mybir.dt.float32
BF16 = mybir.dt.bfloat16
AF = mybir.ActivationFunctionType
ALU = mybir.AluOpType
AX = mybir.AxisListType

ENGS = ("pe", "act", "dve", "pool", "sp")

D = 1024
KC = 8
FFD = 2816
FC = 22
NSPLIT = 2
FH = FC // NSPLIT
L = 2
NPT = 384
NS = 16
NT = 3
NG = 2
GP = NPT * NT
GT = GP + NS
TTM = NPT + NS
NCOLS = NG * GP + NS
HALO = 256
OWN = 2048
ALPHA = (2.0 * L) ** 0.25
EPS = 1e-5
NEG = -30000.0
RING = 3
LOOK = 1
import os
KSTOP = int(os.environ.get('KSTOP', '99'))
KCROSS = int(os.environ.get('KCROSS', '9'))


class Res:
    __slots__ = ("name", "w", "r")

    def __init__(self, name):
        self.name = name
        self.w = None
        self.r = []


class Prog:
    def __init__(self):
        self.ops = {e: [] for e in ENGS}
        self.cnt = {}
        self.seen = {e: {} for e in ENGS}
        self.semkeys = []
        self.deferred = {}

    def _bump(self, key, inc):
        if key not in self.cnt:
            self.cnt[key] = 0
            self.semkeys.append(key)
        self.cnt[key] += inc
        return (key, self.cnt[key])

    def op(self, eng, fn, reads=(), writes=(), dma=None, ndma=1, arena=False):
        if arena and self.deferred.get(eng):
            self.ops[eng].append((None, self.deferred[eng], None))
            self.deferred[eng] = []
        deps = set()
        for r in reads:
            if r.w is not None:
                deps.add(r.w)
        for w in writes:
            if w.w is not None:
                deps.add(w.w)
            deps.update(w.r)
        best = {}
        for (k, v) in deps:
            if eng == "pe" and k == "e:pe":
                continue
            if v > best.get(k, 0):
                best[k] = v
        waits = []
        seen = self.seen[eng]
        for k, v in best.items():
            if seen.get(k, 0) >= v:
                continue
            seen[k] = v
            waits.append((k, v))
        if dma is not None:
            tok = self._bump("d:" + dma, 16 * ndma)
        else:
            tok = self._bump("e:" + eng, 1)
        self.ops[eng].append((fn, waits, tok))
        for r in reads:
            r.r.append(tok)
        for w in writes:
            w.w = tok
            w.r = []
        return tok

    def barrier(self, full=False):
        allt = dict(self.cnt)
        for e in ENGS:
            if e == "pe" and not full:
                continue
            if e in ("pool", "sp") and not full:
                d = dict(self.deferred.get(e) or [])
                for k, v in allt.items():
                    if v > 0 and v > d.get(k, 0):
                        d[k] = v
                self.deferred[e] = list(d.items())
                continue
            waits = []
            for k, v in allt.items():
                if v > 0 and self.seen[e].get(k, 0) < v:
                    self.seen[e][k] = v
                    waits.append((k, v))
            if waits:
                self.ops[e].append((None, waits, None))

    def emit(self, nc):
        with contextlib.ExitStack() as st:
            sems = {}
            for k in self.semkeys:
                sems[k] = st.enter_context(nc.semaphore(k.replace(":", "_")))
            fw = [(k, v) for k, v in self.cnt.items() if v > 0]
            self.ops["sp"].append((None, fw, None))
            block = st.enter_context(nc.Block())

            def run(engname):
                def body(eng):
                    for (fn, waits, tok) in self.ops[engname]:
                        for (k, v) in waits:
                            eng.wait_ge(sems[k], v)
                        if fn is None:
                            continue
                        ins = fn(eng)
                        if tok[0].startswith("d:"):
                            for i_ in (ins if isinstance(ins, (list, tuple)) else [ins]):
                                i_.then_inc(sems[tok[0]], 16)
                        else:
                            if isinstance(ins, (list, tuple)):
                                ins = ins[-1]
                            ins.then_inc(sems[tok[0]], 1)
                return body

            block.tensor(run("pe"))
            block.scalar(run("act"))
            block.vector(run("dve"))
            block.gpsimd(run("pool"))
            block.sync(run("sp"))


def gcol(l, i, k):
    return ((l * 4 + i) * 8 + k) * 2


def build_nc():
    nc = bass.Bass("TRN2", target_bir_lowering=False)

    def din(name, shape):
        return nc.dram_tensor(name, list(shape), F32, kind="ExternalInput").ap()

    def dout(name, shape):
        return nc.dram_tensor(name, list(shape), F32, kind="ExternalOutput").ap()

    xT = din("xT", [D, NCOLS])
    memT = din("memT", [D, 256])
    w_gu = [din("w_gu1", [L, D, 2 * FFD]), din("w_gu2", [L, D, 2 * FFD])]
    w_dn = [din("w_dn1", [L, FFD, D]), din("w_dn2", [L, FFD, D])]
    w_in = din("w_in", [L, D, 2304])
    w_out = din("w_out", [L, D, D])
    w_cq = din("w_cq", [L, D, D])
    w_mk = din("w_mk", [L, D, D])
    w_mv = din("w_mv", [L, D, D])
    w_co = din("w_co", [L, D, D])
    gvec = din("gvec", [128, L * 4 * 8 * 2])
    convw = din("convw", [128, L * 3 * 4])
    sinkq = din("sinkq", [128, L * 8])
    sinks = din("sinks", [128, L])
    ident = din("ident", [128, 128])
    maskA = din("maskA", [128, 256])
    maskF = din("maskF", [128, 256])
    flag = din("flag", [128, 1])
    stT = din("stT", [128, L * 4 * 2 * NS])
    wk_o = din("wk_o", [L, NS, 128, 128])
    wv_o = din("wv_o", [L, NS, 128, 128])
    wk_s = din("wk_s", [L, 128, 8192])
    wv_s = din("wv_s", [L, 128, 8192])
    cmk = din("cmk", [L, 128, 32768])
    cmv = din("cmv", [L, 128, 32768])

    yT = dout("yT", [D, OWN + NS])
    wkp = dout("wkp", [L, 128, 128])
    wvp = dout("wvp", [L, 128, 128])
    convp = dout("convp", [L, 4, 128, 2])
    mkp = dout("mkp", [L, D, 256])
    mvp = dout("mvp", [L, 256, D])
    wks = dout("wks", [L, NS, 128, 128])
    wvs = dout("wvs", [L, NS, 128, 128])
    convs = dout("convs", [L, 4, 128, 2, NS])

    def dscr(name, shape):
        return nc.dram_tensor(name, list(shape), F32).ap()

    scr_q = dscr("scr_q", [NS, 512])
    scr_k = dscr("scr_k", [NS, 512])
    scr_v = dscr("scr_v", [NS, 512])
    scr_o = dscr("scr_o", [NS, 512])
    scr_qc = dscr("scr_qc", [NS, 4, 2, 256])
    scr_sc = dscr("scr_sc", [128, 128])
    scr_pc = dscr("scr_pc", [128, 128])
    scr_oc = dscr("scr_oc", [128, 256])
    scr_mkt = nc.dram_tensor("scr_mkt", [L, 128, KC * 256], BF16).ap()
    scr_mv = nc.dram_tensor("scr_mv", [L, 128, 2 * D], BF16).ap()

    def sb(name, shape, dt=F32):
        return nc.alloc_sbuf_tensor(name, list(shape), dt)

    X = sb("X", [128, KC, GT])
    Xb = sb("Xb", [128, KC, GT], BF16)
    ring = [sb("ring%d" % i, [128, KC, 512], BF16) for i in range(RING)]
    WVX = sb("WVX", [128, KC, 512], BF16)
    yb = sb("yb", [128, KC, TTM], BF16)
    ysq = sb("ysq", [128, KC, TTM], BF16)
    mean = sb("mean", [128, TTM])
    var = sb("var", [128, TTM])
    rstd = sb("rstd", [128, TTM])
    nmr = sb("nmr", [128, TTM])
    t1 = [sb("t1_%d" % i, [128, TTM]) for i in range(2)]
    sg = [sb("sg_%d" % i, [128, TTM]) for i in range(2)]
    identb = sb("identb", [128, 128], BF16)
    identf = sb("identf", [128, 128])
    onesb = sb("onesb", [128, 128], BF16)
    maskAb = sb("maskAb", [128, 256], BF16)
    maskFb = sb("maskFb", [128, 256], BF16)
    gv = sb("gv", [128, L * 4 * 8 * 2])
    cw = sb("cw", [128, L * 3 * 4])
    sq = sb("sq", [128, L * 8])
    ss = sb("ss", [128, L])
    flg = sb("flg", [128, 1])
    epsc = sb("epsc", [128, 1])
    nsq = sb("nsq", [128, L * 8])
    stS = sb("stS", [128, L * 4 * 2 * NS])
    MKT = sb("MKT", [128, KC, 256], BF16)
    MV = sb("MV", [128, 2, D], BF16)
    KKc = [sb("KKc%d" % l, [128, 2, 128], BF16) for l in range(L)]
    Vpc = [sb("Vpc%d" % l, [128, 512], BF16) for l in range(L)]
    Uc = [sb("Uc%d" % l, [128, 4, 2]) for l in range(L)]
    small = sb("small", [128, 64])
    stg = [sb("stg%d" % i, [128, 512]) for i in range(2)]
    qx = sb("qx", [128, 256])
    ssm = sb("ssm", [128, 258])
    oacc = sb("oacc", [128, 256])
    AR = 17696
    arena = sb("arena", [128, AR])

    def aview(off_words, nwords, dt, pattern=None, **kw):
        v = arena[:, off_words:off_words + nwords]
        if dt == BF16:
            v = v.bitcast(BF16)
        if pattern:
            v = v.rearrange(pattern, **kw)
        return v

    H = aview(0, FH * GT // 2, BF16, "p (j t) -> p j t", j=FH)
    o_ = FH * GT // 2
    WD = [aview(o_ + i * (FH * D // 2), FH * D // 2, BF16, "p (j c) -> p j c", j=FH) for i in range(2)]
    assert o_ + 2 * (FH * D // 2) <= AR
    o_ = 0
    Zb = aview(o_, KC * TTM // 2, BF16, "p (c t) -> p c t", c=KC); o_ += KC * TTM // 2
    U = aview(o_, 4 * (TTM + 2), F32, "p (c t) -> p c t", c=4); oU = o_; o_ += 4 * (TTM + 2)
    QT = aview(o_, 4 * TTM // 2, BF16, "p (c t) -> p c t", c=4); o_ += 4 * TTM // 2
    QcT = aview(oU, KC * TTM // 2, BF16, "p (c t) -> p c t", c=KC)
    assert KC * TTM // 2 <= 4 * (TTM + 2) + 4 * TTM // 2
    KK = aview(o_, 2 * (128 + TTM) // 2, BF16, "p (c t) -> p c t", c=2); o_ += 2 * (128 + TTM) // 2
    Vp = aview(o_, 4 * 512 // 2, BF16, "p (c t) -> p c t", c=4); o_ += 4 * 512 // 2
    Pm = aview(o_, 8 * 256 // 2, BF16, "p (c t) -> p c t", c=8); o_ += 8 * 256 // 2
    PT = aview(o_, 16 * 128 // 2, BF16, "p (c t) -> p c t", c=16); o_ += 16 * 128 // 2
    hcs = aview(o_, TTM, F32); o_ += TTM
    acc = aview(o_, TTM, F32); o_ += TTM
    SBo = o_
    SB = aview(o_, 8256, F32); o_ += 8256
    assert o_ <= AR, o_
    KX = SB.rearrange("p (c d) -> p c d", d=64)
    tokS = SB[0:NS, 0:2048]
    SBh = [aview(SBo + i * 4096, 4096, F32, "p (m d) -> p m d", d=256) for i in range(2)]

    ps = nc.alloc_psum_tensor("ps", [128, 6, 512], F32)
    pst2 = nc.alloc_psum_tensor("pst", [128, 2, 1024], BF16)

    R = {}

    def res(n):
        if n not in R:
            R[n] = Res(n)
        return R[n]

    rX = [[res("X%d_%d" % (t, m)) for m in range(KC)] for t in range(NT)]
    rXb = [res("Xb%d" % t) for t in range(NT)]
    rring = [res("ring%d" % i) for i in range(RING)]
    rps = [res("ps%d" % i) for i in range(6)]
    rpstl = [res("pst0"), res("pst1")]
    rH = [res("H%d" % t) for t in range(NT)]
    rWD = [res("WD%d" % i) for i in range(2)]
    rt1 = [res("t1_%d" % i) for i in range(2)]
    rsg = [res("sg_%d" % i) for i in range(2)]
    rstg = [res("stg%d" % i) for i in range(2)]
    rSB = [res("SB0"), res("SB1")]
    R["tokS"] = rSB[0]

    state = {"reqs": None, "rec": [], "idx": 0, "issued": 0, "bank": 0, "t1": 0, "sg": 0, "stg": 0, "pend": [], "pstat": [], "trim": 0}

    def run_build(P, reqs):
        state.update(reqs=reqs, rec=[], idx=0, issued=0, bank=0, t1=0, sg=0, stg=0, pend=[], pstat=[], trim=0)
        for r_ in R.values():
            r_.w = None
            r_.r = []

        def issue_load(j):
            parts = state["reqs"][j]
            slot = j % RING

            def fn(e, parts=parts, slot=slot):
                out = []
                for (dst0, ncol, src) in parts:
                    out.append(e.dma_start(out=ring[slot][:, :, dst0:dst0 + ncol],
                                           in_=src.rearrange("(k p) c -> p k c", p=128)))
                return out
            P.op("pool", fn, writes=[rring[slot]], dma="ring%d" % slot, ndma=len(parts))

        def getw(parts):
            i = state["idx"]
            state["idx"] += 1
            state["rec"].append(parts)
            if state["reqs"] is not None:
                while state["issued"] < min(i + 1 + LOOK, len(state["reqs"])):
                    issue_load(state["issued"])
                    state["issued"] += 1
            return ring[i % RING], rring[i % RING]

        def bank():
            b = state["bank"]
            state["bank"] = (b + 1) % 4
            return b

        def nxt(key, n=2):
            v = state[key]
            state[key] = (v + 1) % n
            return v

        def tile_cols(g, t):
            n = NPT + (NS if (g == NG - 1 and t == NT - 1) else 0)
            if g == 0 and t == 0 and state["trim"]:
                return state["trim"], n - state["trim"]
            return t * NPT, n

        def ld(eng, dst, src, r_, key):
            P.op(eng, lambda e: e.dma_start(out=dst, in_=src), writes=[r_], dma=key)

        ld("pool", identb[:], ident, res("identb"), "c0")
        ld("sp", identf[:], ident, res("identf"), "c1")
        ld("pool", maskAb[:], maskA, res("maskAb"), "c2")
        ld("pool", maskFb[:], maskF, res("maskFb"), "c3")
        ld("sp", gv[:], gvec, res("gv"), "c4")
        ld("sp", cw[:], convw, res("cw"), "c5")
        ld("sp", sq[:], sinkq, res("sq"), "c6")
        ld("sp", ss[:], sinks, res("ss"), "c7")
        ld("sp", flg[:], flag, res("flg"), "c8")
        ld("sp", stS[:], stT, res("stS"), "c9")
        P.op("dve", lambda e: e.memset(onesb[:], 1.0 / D), writes=[res("onesb")])
        P.op("dve", lambda e: e.memset(epsc[:], EPS), writes=[res("epsc")])
        P.op("dve", lambda e: e.tensor_scalar(out=nsq[:], in0=sq[:], scalar1=-1.0, scalar2=None, op0=ALU.mult), reads=[res("sq")], writes=[res("nsq")])
        P.op("dve", lambda e: e.memset(WVX[:], 0.0), writes=[res("WVX")])
        shv = arena[:, 0:2048].rearrange("p (b x) -> p b x", x=128)
        for l in range(L):
            for (src, dst, nm) in ((wk_o, wks, "k"), (wv_o, wvs, "v")):
                P.op("sp", lambda e, src=src, l=l: e.dma_start(out=shv, in_=src[l].rearrange("b k x -> k b x")),
                     writes=[res("shv")], dma="shin", arena=True)
                P.op("sp", lambda e, dst=dst, l=l: e.dma_start(out=dst[l][:, 0:127, :].rearrange("b k x -> k b x"), in_=shv[1:128, :, :]),
                     reads=[res("shv")], writes=[res("o_w%s%d" % (nm, l))], dma="shout")
        P.barrier(full=True)

        def drain(k=1):
            while k > 0 and state["pend"]:
                _, fn_ = state["pend"].pop(0)
                fn_()
                k -= 1

        def need(t):
            if any(tt == t for tt, _ in state["pend"]):
                drain(10 ** 6)

        def ln_piece(g, t, m):
            c0, n = tile_cols(g, t)
            sl = slice(c0, c0 + n)
            P.op("act", lambda e: e.activation(out=yb[:, m, 0:n], in_=X[:, m, sl], func=AF.Copy),
                 reads=[rX[t][m]], writes=[res("yb%d" % m)])
            P.op("act", lambda e: e.activation(out=ysq[:, m, 0:n], in_=X[:, m, sl], func=AF.Square),
                 reads=[rX[t][m]], writes=[res("ysq%d" % m)])

            def stats(e):
                e.matmul(ps[:, 4, 0:n], lhsT=onesb[:], rhs=yb[:, m, 0:n], start=(m == 0), stop=(m == KC - 1))
                return e.matmul(ps[:, 5, 0:n], lhsT=onesb[:], rhs=ysq[:, m, 0:n], start=(m == 0), stop=(m == KC - 1))
            state["pstat"].append(lambda: P.op("pe", stats, reads=[res("onesb"), res("yb%d" % m), res("ysq%d" % m)], writes=[rps[4], rps[5]]))
            while len(state["pstat"]) > 2:
                state["pstat"].pop(0)()

        def layer_norm(g, t, li):
            while state["pstat"]:
                state["pstat"].pop(0)()
            drain(10 ** 6)
            c0, n = tile_cols(g, t)
            sl = slice(c0, c0 + n)
            P.op("dve", lambda e: e.tensor_copy(out=mean[:, 0:n], in_=ps[:, 4, 0:n]), reads=[rps[4]], writes=[res("mean")])
            P.op("dve", lambda e: e.tensor_tensor(out=var[:, 0:n], in0=mean[:, 0:n], in1=mean[:, 0:n], op=ALU.mult),
                 reads=[res("mean")], writes=[res("var")])
            P.op("dve", lambda e: e.tensor_tensor(out=var[:, 0:n], in0=ps[:, 5, 0:n], in1=var[:, 0:n], op=ALU.subtract),
                 reads=[rps[5], res("var")], writes=[res("var")])
            P.op("act", lambda e: e.activation(out=var[:, 0:n], in_=var[:, 0:n], func=AF.Ln, bias=epsc[:, 0:1]), reads=[res("var"), res("epsc")], writes=[res("var")])
            P.op("act", lambda e: e.activation(out=rstd[:, 0:n], in_=var[:, 0:n], func=AF.Exp, scale=-0.5), reads=[res("var")], writes=[res("rstd")])
            l_, i_ = li

            def apply(m):
                b_ = nxt("t1")
                gc = gcol(l_, i_, m)
                P.op("dve", lambda e: e.tensor_tensor(out=t1[b_][:, 0:n], in0=X[:, m, sl], in1=mean[:, 0:n], op=ALU.subtract),
                     reads=[rX[t][m], res("mean")], writes=[rt1[b_]])
                P.op("dve", lambda e: e.tensor_tensor(out=t1[b_][:, 0:n], in0=t1[b_][:, 0:n], in1=rstd[:, 0:n], op=ALU.mult),
                     reads=[rt1[b_], res("rstd")], writes=[rt1[b_]])
                P.op("act", lambda e: e.activation(out=X[:, m, sl], in_=t1[b_][:, 0:n], func=AF.Identity,
                                                   bias=gv[:, gc + 1:gc + 2], scale=gv[:, gc:gc + 1]),
                     reads=[rt1[b_], res("gv")], writes=[rX[t][m]])
                P.op("act", lambda e: e.activation(out=Xb[:, m, sl], in_=t1[b_][:, 0:n], func=AF.Identity,
                                                   bias=gv[:, gc + 1:gc + 2], scale=gv[:, gc:gc + 1]),
                     reads=[rt1[b_], res("gv")], writes=[rXb[t]])
            for m in range(KC):
                state["pend"].append((t, (lambda m=m: apply(m))))

        def proj_chunk(wt, wr, wcol, g, t, extra_reads=(), rhs_src=None, rhs_res=None):
            c0, n = tile_cols(g, t)
            b = bank()
            src = Xb if rhs_src is None else rhs_src
            off = c0 if rhs_src is None else 0

            def fn(e):
                for k in range(KC):
                    ins = e.matmul(ps[:, b, 0:n], lhsT=wt[:, k, wcol:wcol + 128], rhs=src[:, k, off:off + n],
                                   start=(k == 0), stop=(k == KC - 1))
                return ins
            need(t)
            P.op("pe", fn, reads=[wr, rXb[t] if rhs_res is None else rhs_res] + list(extra_reads), writes=[rps[b]])
            drain(1)
            return b

        def resid_evac(g, t, m, b, first=True, final=True):
            need(t)
            c0, n = tile_cols(g, t)
            sl = slice(c0, c0 + n)
            if first:
                P.op("dve", lambda e: e.scalar_tensor_tensor(out=X[:, m, sl], in0=X[:, m, sl], scalar=ALPHA, in1=ps[:, b, 0:n],
                                                             op0=ALU.mult, op1=ALU.add),
                     reads=[rps[b], rX[t][m]], writes=[rX[t][m]])
            else:
                P.op("dve", lambda e: e.tensor_tensor(out=X[:, m, sl], in0=X[:, m, sl], in1=ps[:, b, 0:n], op=ALU.add),
                     reads=[rps[b], rX[t][m]], writes=[rX[t][m]])
            if final:
                ln_piece(g, t, m)

        def ffn(g, l, which):
            state["trim"] = {(0, 0): 0, (0, 1): 128, (1, 0): 128, (1, 1): 256}[(l, which)] if g == 0 else 0
            try:
                ffn_body(g, l, which)
            finally:
                state["trim"] = 0

        def ffn_body(g, l, which):
            wgu = w_gu[which][l]
            wdn = w_dn[which][l]
            for half in range(NSPLIT):
                j0 = half * FH
                def wdload(e, half=half, j0=j0):
                    outs = []
                    for (a, b_) in ((0, 3), (3, 6), (6, 9), (9, FH)):
                        outs.append(e.dma_start(out=WD[half][:, a:b_, :],
                                                in_=wdn[(j0 + a) * 128:(j0 + b_) * 128, :].rearrange("(j p) c -> p j c", p=128)))
                    return outs
                P.op("pool", wdload, writes=[rWD[half]], dma="wd%d" % half, ndma=4, arena=True)
                jj = 0
                while jj < FH:
                    nch = min(2, FH - jj)
                    ca = (j0 + jj) * 128
                    wt, wr = getw([(0, nch * 128, wgu[:, ca:ca + nch * 128]),
                                   (256, nch * 128, wgu[:, FFD + ca:FFD + ca + nch * 128])])
                    for t in range(NT):
                        c0, n = tile_cols(g, t)
                        for cc in range(nch):
                            bg_ = proj_chunk(wt, wr, cc * 128, g, t)
                            bu_ = proj_chunk(wt, wr, 256 + cc * 128, g, t)
                            s_ = nxt("sg")
                            P.op("act", lambda e, s_=s_, bg_=bg_, n=n: e.activation(out=sg[s_][:, 0:n], in_=ps[:, bg_, 0:n], func=AF.Silu),
                                 reads=[rps[bg_]], writes=[rsg[s_]])
                            jl = jj + cc
                            P.op("dve", lambda e, s_=s_, bu_=bu_, n=n, jl=jl, c0=c0: e.scalar_tensor_tensor(
                                out=H[:, jl, c0:c0 + n], in0=sg[s_][:, 0:n], scalar=0.5, in1=ps[:, bu_, 0:n], op0=ALU.mult, op1=ALU.mult),
                                reads=[rsg[s_], rps[bu_]], writes=[rH[t]])
                    jj += nch
                for t in range(NT):
                    c0, n = tile_cols(g, t)
                    for m in range(KC):
                        b = bank()

                        def fn(e, b=b, m=m, c0=c0, n=n, half=half):
                            for j in range(FH):
                                ins = e.matmul(ps[:, b, 0:n], lhsT=WD[half][:, j, m * 128:(m + 1) * 128], rhs=H[:, j, c0:c0 + n],
                                               start=(j == 0), stop=(j == FH - 1))
                            return ins
                        P.op("pe", fn, reads=[rWD[half], rH[t]], writes=[rps[b]])
                        resid_evac(g, t, m, b, first=(half == 0), final=(half == NSPLIT - 1))
                        drain(1)
                    if half == NSPLIT - 1:
                        layer_norm(g, t, (l, 0 if which == 0 else 3))

        def mix(g, l):
            win = w_in[l]
            last_g = (g == NG - 1)
            if g == 0:
                P.op("dve", lambda e: e.memset(KK[:, :, 0:128], 0.0), writes=[res("KK")])
                P.op("dve", lambda e: e.memset(Vp[:, 0, :], 0.0), writes=[res("Vp")])
                P.op("dve", lambda e: e.memset(U[:, :, 0:2], 0.0), writes=[res("U")])
            else:
                P.op("dve", lambda e: e.tensor_copy(out=KK[:, :, 0:128], in_=KKc[l][:]), reads=[res("KKc%d" % l)], writes=[res("KK")])
                P.op("dve", lambda e: e.tensor_copy(out=Vp[:, 0, :], in_=Vpc[l][:]), reads=[res("Vpc%d" % l)], writes=[res("Vp")])
                P.op("dve", lambda e: e.tensor_copy(out=U[:, :, 0:2], in_=Uc[l][:]), reads=[res("Uc%d" % l)], writes=[res("U")])

            def wvx(e):
                outs = []
                for (dst, kv) in ((0, 0), (192, 0), (256, 1), (448, 1)):
                    outs.append(e.dma_start(out=WVX[:, :, dst:dst + 64],
                                            in_=win[:, 2176 + kv * 64:2176 + (kv + 1) * 64].rearrange("(k p) c -> p k c", p=128)))
                return outs
            P.op("pool", wvx, writes=[res("WVX")], dma="wvx", ndma=4)

            def mix_tile(t):
                c0, n = tile_cols(g, t)
                has_s = n > NPT
                w_hc, r_hc = getw([(0, 512, win[:, 1024:1536])])
                w_cg, r_cg = getw([(0, 512, win[:, 512:1024])])
                for c in range(4):
                    b1 = proj_chunk(w_hc, r_hc, c * 128, g, t)
                    P.op("act", lambda e, b1=b1: e.activation(out=hcs[:, 0:n], in_=ps[:, b1, 0:n], func=AF.Copy),
                         reads=[rps[b1]], writes=[res("hcs")])
                    b2 = proj_chunk(w_cg, r_cg, c * 128, g, t)
                    P.op("dve", lambda e, b2=b2, c=c: e.tensor_tensor(out=U[:, c, 2:2 + n], in0=ps[:, b2, 0:n], in1=hcs[:, 0:n], op=ALU.mult),
                         reads=[rps[b2], res("hcs")], writes=[res("U")])
                if g == 0 and t == 0:
                    P.op("dve", lambda e: e.tensor_scalar(out=U[:, :, 2 + HALO - 2:2 + HALO], in0=U[:, :, 2 + HALO - 2:2 + HALO],
                                                          scalar1=flg[:, 0:1], scalar2=None, op0=ALU.mult),
                         reads=[res("U"), res("flg")], writes=[res("U")])
                w_bg, r_bg = getw([(0, 512, win[:, 0:512])])
                for c in range(4):
                    b3 = proj_chunk(w_bg, r_bg, c * 128, g, t)
                    w0 = cw[:, (l * 3 + 0) * 4 + c:(l * 3 + 0) * 4 + c + 1]
                    w1 = cw[:, (l * 3 + 1) * 4 + c:(l * 3 + 1) * 4 + c + 1]
                    w2 = cw[:, (l * 3 + 2) * 4 + c:(l * 3 + 2) * 4 + c + 1]
                    P.op("dve", lambda e, c=c, w0=w0: e.tensor_scalar(out=acc[:, 0:NPT], in0=U[:, c, 0:NPT], scalar1=w0, scalar2=None, op0=ALU.mult),
                         reads=[res("U"), res("cw")], writes=[res("acc")])
                    P.op("dve", lambda e, c=c, w1=w1: e.scalar_tensor_tensor(out=acc[:, 0:NPT], in0=U[:, c, 1:1 + NPT], scalar=w1, in1=acc[:, 0:NPT],
                                                                             op0=ALU.mult, op1=ALU.add),
                         reads=[res("U"), res("acc")], writes=[res("acc")])
                    P.op("dve", lambda e, c=c, w2=w2: e.scalar_tensor_tensor(out=acc[:, 0:NPT], in0=U[:, c, 2:2 + NPT], scalar=w2, in1=acc[:, 0:NPT],
                                                                             op0=ALU.mult, op1=ALU.add),
                         reads=[res("U"), res("acc")], writes=[res("acc")])
                    if has_s:
                        so = ((l * 4 + c) * 2) * NS
                        P.op("dve", lambda e, so=so, w0=w0: e.tensor_scalar(out=acc[:, NPT:n], in0=stS[:, so:so + NS], scalar1=w0, scalar2=None, op0=ALU.mult),
                             reads=[res("stS"), res("acc")], writes=[res("acc")])
                        P.op("dve", lambda e, so=so, w1=w1: e.scalar_tensor_tensor(out=acc[:, NPT:n], in0=stS[:, so + NS:so + 2 * NS], scalar=w1, in1=acc[:, NPT:n],
                                                                                   op0=ALU.mult, op1=ALU.add),
                             reads=[res("stS"), res("acc")], writes=[res("acc")])
                        P.op("dve", lambda e, c=c, w2=w2: e.scalar_tensor_tensor(out=acc[:, NPT:n], in0=U[:, c, 2 + NPT:2 + n], scalar=w2, in1=acc[:, NPT:n],
                                                                                 op0=ALU.mult, op1=ALU.add),
                             reads=[res("U"), res("acc")], writes=[res("acc")])
                    P.op("dve", lambda e, c=c, b3=b3: e.tensor_tensor(out=Zb[:, c, 0:n], in0=acc[:, 0:n], in1=ps[:, b3, 0:n], op=ALU.mult),
                         reads=[res("acc"), rps[b3]], writes=[res("Zb")])
                if has_s:
                    def cs(e):
                        outs = []
                        for c in range(4):
                            so = ((l * 4 + c) * 2) * NS
                            outs.append(e.dma_start(out=convs[l, c, :, 0, :], in_=stS[:, so + NS:so + 2 * NS]))
                            outs.append(e.dma_start(out=convs[l, c, :, 1, :], in_=U[:, c, 2 + NPT:2 + n]))
                        return outs
                    P.op("sp", cs, reads=[res("U"), res("stS")], writes=[res("o_convs")], dma="o_convs", ndma=8)
                if last_g and t == NT - 1:
                    def cp(e):
                        outs = []
                        for c in range(4):
                            outs.append(e.dma_start(out=convp[l, c, :, :], in_=U[:, c, NPT:NPT + 2]))
                        return outs
                    P.op("sp", cp, reads=[res("U")], writes=[res("o_convp")], dma="o_convp", ndma=4)
                w_q, r_q = getw([(0, 512, win[:, 1536:2048])])
                for c in range(4):
                    b4 = proj_chunk(w_q, r_q, c * 128, g, t)
                    P.op("act", lambda e, b4=b4, c=c: e.activation(out=QT[:, c, 0:n], in_=ps[:, b4, 0:n], func=AF.Copy),
                         reads=[rps[b4]], writes=[res("QT")])
                w_k, r_k = getw([(0, 64, win[:, 2048:2112]), (64, 64, win[:, 2048:2112]),
                                 (128, 64, win[:, 2112:2176]), (192, 64, win[:, 2112:2176])])
                for kv in range(2):
                    b5 = proj_chunk(w_k, r_k, kv * 128, g, t)
                    P.op("act", lambda e, b5=b5, kv=kv: e.activation(out=KK[:, kv, 128:128 + n], in_=ps[:, b5, 0:n], func=AF.Copy),
                         reads=[rps[b5]], writes=[res("KK")])
                    if last_g and t == NT - 1:
                        s_ = 0
                        P.op("act", lambda e, b5=b5, kv=kv: e.activation(out=stg[0][kv * 64:(kv + 1) * 64, 0:128],
                                                                          in_=ps[kv * 64:(kv + 1) * 64, b5, NPT - 128:NPT], func=AF.Copy),
                             reads=[rps[b5]], writes=[rstg[0]])
                if last_g and t == NT - 1:
                    P.op("sp", lambda e: e.dma_start(out=wkp[l], in_=stg[0][:, 0:128]), reads=[rstg[0]], writes=[res("o_wkp")], dma="o_wkp")
                need(t)
                for blk in range(3):
                    bv = bank()

                    def fnv(e, bv=bv, blk=blk):
                        for k in range(KC):
                            ins = e.matmul(ps[:, bv, :], lhsT=Xb[:, k, c0 + blk * 128:c0 + (blk + 1) * 128], rhs=WVX[:, k, :],
                                           start=(k == 0), stop=(k == KC - 1))
                        return ins
                    P.op("pe", fnv, reads=[rXb[t], res("WVX")], writes=[rps[bv]])
                    P.op("act", lambda e, bv=bv, blk=blk: e.activation(out=Vp[:, 1 + blk, :], in_=ps[:, bv, :], func=AF.Copy),
                         reads=[rps[bv]], writes=[res("Vp")])
                    if last_g and t == NT - 1 and blk == 2:
                        P.op("act", lambda e, bv=bv: e.activation(out=stg[1][:, 0:64], in_=ps[:, bv, 0:64], func=AF.Copy), reads=[rps[bv]], writes=[rstg[1]])
                        P.op("act", lambda e, bv=bv: e.activation(out=stg[1][:, 64:128], in_=ps[:, bv, 256:320], func=AF.Copy), reads=[rps[bv]], writes=[rstg[1]])
                        P.op("sp", lambda e: e.dma_start(out=wvp[l], in_=stg[1][:, 0:128]), reads=[rstg[1]], writes=[res("o_wvp")], dma="o_wvp")
                if has_s:
                    sample_swa(g, t, l)
                mx = small[:, 0:8]
                nb = small[:, 8:16]
                sm = small[:, 16:24]
                es = small[:, 24:32]
                sk = sq[:, l * 8:(l + 1) * 8]
                nsk = nsq[:, l * 8:(l + 1) * 8]
                S4 = ps[:, 0:4, :].rearrange("p b (h c) -> p (b h) c", h=2)

                def st_S(blk):
                    qs = slice(blk * 128, (blk + 1) * 128)
                    ks = slice(blk * 128, blk * 128 + 256)
                    first = (g == 0 and t == 0 and blk == 2)
                    mk_ = maskFb if first else maskAb
                    mr_ = res("maskFb") if first else res("maskAb")

                    def fs(e):
                        for h in range(8):
                            hp = (h % 2) * 64
                            o = ps[:, h // 2, (h % 2) * 256:(h % 2) * 256 + 256]
                            e.matmul(o, lhsT=QT[hp:hp + 64, h // 2, qs], rhs=KK[hp:hp + 64, h // 4, ks], start=True, stop=False)
                            ins = e.matmul(o, lhsT=identb[:], rhs=mk_[:], start=False, stop=True)
                        return ins
                    P.op("pe", fs, reads=[res("QT"), res("KK"), res("identb"), mr_], writes=[rps[0], rps[1], rps[2], rps[3]])
                    state["bank"] = 0

                def st_A1(blk):
                    P.op("dve", lambda e: e.tensor_reduce(out=mx, in_=S4, axis=AX.X, op=ALU.max),
                         reads=[rps[0], rps[1], rps[2], rps[3]], writes=[res("small")])
                    P.op("dve", lambda e: e.scalar_tensor_tensor(out=nb, in0=mx, scalar=-0.125, in1=nsk, op0=ALU.mult, op1=ALU.min),
                         reads=[res("small"), res("nsq")], writes=[res("small")])
                    P.op("dve", lambda e: e.tensor_tensor(out=es, in0=sk, in1=nb, op=ALU.add), reads=[res("small"), res("sq")], writes=[res("small2")])

                def st_A2(blk):
                    def fe(e):
                        for h in range(8):
                            ins = e.activation(out=Pm[:, h, :], in_=ps[:, h // 2, (h % 2) * 256:(h % 2) * 256 + 256], func=AF.Exp,
                                               bias=nb[:, h:h + 1], scale=0.125, accum_out=sm[:, h:h + 1])
                        return ins
                    P.op("act", fe, reads=[rps[0], rps[1], rps[2], rps[3], res("small")], writes=[res("Pm"), res("small3")])
                    P.op("act", lambda e: e.activation(out=es, in_=es, func=AF.Exp), reads=[res("small2")], writes=[res("small2")])

                def st_B(blk):
                    P.op("dve", lambda e: e.tensor_tensor(out=sm, in0=sm, in1=es, op=ALU.add), reads=[res("small3"), res("small2")], writes=[res("small3")])
                    P.op("dve", lambda e: e.reciprocal(out=sm, in_=sm), reads=[res("small3")], writes=[res("small3")])

                    def fnorm(e):
                        for h in range(8):
                            ins = e.tensor_scalar(out=Pm[:, h, :], in0=Pm[:, h, :], scalar1=sm[:, h:h + 1], scalar2=None, op0=ALU.mult)
                        return ins
                    P.op("dve", fnorm, reads=[res("small3"), res("Pm")], writes=[res("Pm")])

                def st_T(blk):
                    qs = slice(blk * 128, (blk + 1) * 128)
                    for rr in range(2):
                        def ftr(e, rr=rr):
                            for hh in range(4):
                                h = rr * 4 + hh
                                for kb in range(2):
                                    ins = e.transpose(pst2[:, rr, (hh * 2 + kb) * 128:(hh * 2 + kb + 1) * 128], Pm[:, h, kb * 128:(kb + 1) * 128], identb[:])
                            return ins
                        P.op("pe", ftr, reads=[res("Pm"), res("identb")], writes=[rpstl[rr]])
                        P.op("act", lambda e, rr=rr: e.activation(out=PT[:, rr * 8:(rr + 1) * 8, :],
                                                                  in_=pst2[:, rr, :].rearrange("p (a q) -> p a q", q=128), func=AF.Copy),
                             reads=[rpstl[rr]], writes=[res("PT%d" % rr)])

                    def fpv(e):
                        for c in range(4):
                            kv = c // 2
                            i_ = 0
                            for hh in range(2):
                                h = 2 * c + hh
                                vcol = kv * 256 + hh * 128
                                for kb in range(2):
                                    ins = e.matmul(ps[:, 4, c * 128:(c + 1) * 128], lhsT=Vp[:, blk + kb, vcol:vcol + 128], rhs=PT[:, h * 2 + kb, :],
                                                   start=(i_ == 0), stop=(i_ == 3))
                                    i_ += 1
                        return ins
                    P.op("pe", fpv, reads=[res("Vp"), res("PT0"), res("PT1")], writes=[rps[4]])
                    P.op("act", lambda e: e.activation(out=Zb[:, 4:8, qs], in_=ps[:, 4, :].rearrange("p (c q) -> p c q", q=128), func=AF.Copy),
                         reads=[rps[4]], writes=[res("Zb")])

                blk0 = {0: 1, 1: 2}[l] if (g == 0 and t == 0) else 0
                st_S(blk0)
                st_A1(blk0)
                st_A2(blk0)
                for blk in range(blk0, 3):
                    if blk + 1 < 3:
                        st_S(blk + 1)
                    st_B(blk)
                    if blk + 1 < 3:
                        st_A1(blk + 1)
                    st_T(blk)
                    if blk + 1 < 3:
                        st_A2(blk + 1)
                w_o = [getw([(0, 512, w_out[l][:, 0:512])]), getw([(0, 512, w_out[l][:, 512:1024])])]
                for m in range(KC):
                    wt, wr = w_o[m // 4]
                    b = proj_chunk(wt, wr, (m % 4) * 128, g, t, rhs_src=Zb, rhs_res=res("Zb"))
                    resid_evac(g, t, m, b)
                layer_norm(g, t, (l, 1))
                P.op("dve", lambda e: e.tensor_copy(out=KK[:, :, 0:128], in_=KK[:, :, NPT:NPT + 128]), reads=[res("KK")], writes=[res("KK")])
                P.op("dve", lambda e: e.tensor_copy(out=Vp[:, 0, :], in_=Vp[:, 3, :]), reads=[res("Vp")], writes=[res("Vp")])
                P.op("dve", lambda e: e.tensor_copy(out=U[:, :, 0:2], in_=U[:, :, NPT:NPT + 2]), reads=[res("U")], writes=[res("U")])
            for t in range(NT):
                mix_tile(t)
            if g == 0:
                P.op("dve", lambda e: e.tensor_copy(out=KKc[l][:], in_=KK[:, :, 0:128]), reads=[res("KK")], writes=[res("KKc%d" % l)])
                P.op("dve", lambda e: e.tensor_copy(out=Vpc[l][:], in_=Vp[:, 0, :]), reads=[res("Vp")], writes=[res("Vpc%d" % l)])
                P.op("dve", lambda e: e.tensor_copy(out=Uc[l][:], in_=U[:, :, 0:2]), reads=[res("U")], writes=[res("Uc%d" % l)])

        def tok_major(g, t, wparts_list, ncol_list, dst_col0):
            need(t)
            c0, n = tile_cols(g, t)
            col = dst_col0
            for parts, ncol in zip(wparts_list, ncol_list):
                wt, wr = getw(parts)
                b = bank()

                def fn(e, wt=wt, b=b, ncol=ncol):
                    for k in range(KC):
                        ins = e.matmul(ps[0:NS, b, 0:ncol], lhsT=Xb[:, k, c0 + NPT:c0 + n], rhs=wt[:, k, 0:ncol],
                                       start=(k == 0), stop=(k == KC - 1))
                    return ins
                P.op("pe", fn, reads=[wr, rXb[t]], writes=[rps[b]])
                P.op("act", lambda e, b=b, ncol=ncol, col=col: e.activation(out=tokS[:, col:col + ncol], in_=ps[0:NS, b, 0:ncol], func=AF.Copy),
                     reads=[rps[b]], writes=[res("tokS")])
                col += ncol

        def to_feature_major(nchunks, src_col0, zc0, zcols):
            def ftr(e):
                for c in range(nchunks):
                    ins = e.transpose(ps[:, 4, c * NS:(c + 1) * NS], tokS[:, src_col0 + c * 128:src_col0 + (c + 1) * 128], identf[0:NS, 0:NS])
                return ins
            P.op("pe", ftr, reads=[res("tokS"), res("identf")], writes=[rps[4]])
            P.op("act", lambda e: e.activation(out=Zb[:, zc0:zc0 + nchunks, zcols],
                                               in_=ps[:, 4, 0:nchunks * NS].rearrange("p (c s) -> p c s", s=NS), func=AF.Copy),
                 reads=[rps[4]], writes=[res("Zb")])

        def sample_swa(g, t, l):
            win = w_in[l]
            c0, n = tile_cols(g, t)
            krep = [(h * 64, 64, win[:, 2048 + (h // 4) * 64:2048 + (h // 4 + 1) * 64]) for h in range(8)]
            vrep = [(h * 64, 64, win[:, 2176 + (h // 4) * 64:2176 + (h // 4 + 1) * 64]) for h in range(8)]
            tok_major(g, t, [[(0, 512, win[:, 1536:2048])], krep, vrep], [512, 512, 512], 0)

            def f1(e):
                return [e.dma_start(out=scr_q, in_=tokS[:, 0:512]),
                        e.dma_start(out=scr_k, in_=tokS[:, 512:1024]),
                        e.dma_start(out=scr_v, in_=tokS[:, 1024:1536]),
                        e.dma_start(out=wks[l][:, 127, :].rearrange("b (kv d) -> b kv d", d=64),
                                    in_=tokS[:, 512:1024].rearrange("b (kv x) -> b kv x", kv=2)[:, :, 0:64]),
                        e.dma_start(out=wvs[l][:, 127, :].rearrange("b (kv d) -> b kv d", d=64),
                                    in_=tokS[:, 1024:1536].rearrange("b (kv x) -> b kv x", kv=2)[:, :, 0:64])]
            P.op("sp", f1, reads=[res("tokS")], writes=[res("scr_qkv")], dma="scr1", ndma=5)

            sc = ssm[:, 0:129]
            P.op("sp", lambda e: e.dma_start(out=qx[:, 0:64], in_=scr_q.rearrange("b (h d) -> (b h) d", d=64)),
                 reads=[res("scr_qkv")], writes=[res("qx")], dma="scr2")
            wkv = wk_s[l].rearrange("p (c d) -> p c d", d=64)
            wvv = wv_s[l].rearrange("p (c d) -> p c d", d=64)

            def kvload(src, scr_new, hi, key):
                k0 = hi * 64

                def fl(e):
                    outs = [e.dma_start(out=KX[:, k0:k0 + 64, :], in_=src[:, k0:k0 + 64, :])]
                    if hi == 1:
                        outs.append(e.dma_start(out=KX[:, 128, :], in_=scr_new.rearrange("b (h d) -> (b h) d", d=64)))
                    return outs
                P.op("sp", fl, reads=[res("scr_qkv")], writes=[rSB[hi]], dma="%s%d" % (key, hi), ndma=1 + hi, arena=True)
            kvload(wkv, scr_k, 0, "swk")
            kvload(wkv, scr_k, 1, "swk")
            for hi in range(2):
                k0 = hi * 64
                tot = 64 + hi
                P.op("dve", lambda e, tot=tot, k0=k0: e.tensor_tensor(out=KX[:, k0:k0 + tot, :], in0=KX[:, k0:k0 + tot, :],
                                                                      in1=qx[:, 0:64].unsqueeze(1).to_broadcast([128, tot, 64]), op=ALU.mult),
                     reads=[res("qx"), rSB[hi]], writes=[rSB[hi]])
                P.op("dve", lambda e, tot=tot, k0=k0: e.tensor_reduce(out=sc[:, k0:k0 + tot], in_=KX[:, k0:k0 + tot, :], axis=AX.X, op=ALU.add),
                     reads=[rSB[hi]], writes=[res("ssm")])
                kvload(wvv, scr_v, hi, "swv")
            mx = small[:, 32:33]
            nb = small[:, 33:34]
            sm = small[:, 34:35]
            es = small[:, 35:36]
            sk = ss[:, l:l + 1]
            P.op("dve", lambda e: e.tensor_reduce(out=mx, in_=sc, axis=AX.X, op=ALU.max), reads=[res("ssm")], writes=[res("smallS")])
            P.op("dve", lambda e: e.scalar_tensor_tensor(out=mx, in0=mx, scalar=0.125, in1=sk, op0=ALU.mult, op1=ALU.max),
                 reads=[res("smallS"), res("ss")], writes=[res("smallS")])
            P.op("dve", lambda e: e.tensor_scalar(out=nb, in0=mx, scalar1=-1.0, scalar2=None, op0=ALU.mult), reads=[res("smallS")], writes=[res("smallS")])
            P.op("dve", lambda e: e.tensor_tensor(out=es, in0=sk, in1=nb, op=ALU.add), reads=[res("smallS"), res("ss")], writes=[res("smallS")])
            P.op("act", lambda e: e.activation(out=sc, in_=sc, func=AF.Exp, bias=nb, scale=0.125), reads=[res("ssm"), res("smallS")], writes=[res("ssm")])
            P.op("act", lambda e: e.activation(out=es, in_=es, func=AF.Exp), reads=[res("smallS")], writes=[res("smallS")])
            P.op("dve", lambda e: e.tensor_reduce(out=sm, in_=sc, axis=AX.X, op=ALU.add), reads=[res("ssm")], writes=[res("smallS")])
            P.op("dve", lambda e: e.tensor_tensor(out=sm, in0=sm, in1=es, op=ALU.add), reads=[res("smallS")], writes=[res("smallS")])
            P.op("dve", lambda e: e.reciprocal(out=sm, in_=sm), reads=[res("smallS")], writes=[res("smallS")])
            P.op("dve", lambda e: e.tensor_scalar(out=sc, in0=sc, scalar1=sm, scalar2=None, op0=ALU.mult), reads=[res("ssm"), res("smallS")], writes=[res("ssm")])
            for hi in range(2):
                k0 = hi * 64
                tot = 64 + hi
                P.op("dve", lambda e, tot=tot, k0=k0: e.tensor_tensor(out=KX[:, k0:k0 + tot, :], in0=KX[:, k0:k0 + tot, :],
                                                                      in1=sc[:, k0:k0 + tot].unsqueeze(2).to_broadcast([128, tot, 64]), op=ALU.mult),
                     reads=[res("ssm"), rSB[hi]], writes=[rSB[hi]])
                dst = oacc[:, 0:64] if hi == 0 else qx[:, 64:128]
                P.op("dve", lambda e, tot=tot, k0=k0, dst=dst: e.tensor_reduce(out=dst, in_=KX[:, k0:k0 + tot, :].rearrange("p c d -> p d c"), axis=AX.X, op=ALU.add),
                     reads=[rSB[hi]], writes=[res("oacc"), res("qx")])
            P.op("dve", lambda e: e.tensor_tensor(out=oacc[:, 0:64], in0=oacc[:, 0:64], in1=qx[:, 64:128], op=ALU.add),
                 reads=[res("oacc"), res("qx")], writes=[res("oacc")])
            P.op("sp", lambda e: e.dma_start(out=scr_o.rearrange("b (h d) -> (b h) d", d=64), in_=oacc[:, 0:64]),
                 reads=[res("oacc")], writes=[res("scr_o")], dma="scr4")
            P.op("sp", lambda e: e.dma_start(out=tokS[:, 1536:2048], in_=scr_o), reads=[res("scr_o")], writes=[res("tokS")], dma="scr5", arena=True)
            to_feature_major(4, 1536, 4, slice(NPT, n))

        def cross(g, l):
            if g > 0:
                P.op("sp", lambda e: [e.dma_start(out=MKT[:].rearrange("p c t -> p (c t)"), in_=scr_mkt[l]),
                                      e.dma_start(out=MV[:].rearrange("p c t -> p (c t)"), in_=scr_mv[l])],
                     reads=[res("scr_mk%d" % l)], writes=[res("MKT"), res("MV")], dma="mkin", ndma=2)
            for m in range(KC if (KCROSS >= 1 and g == 0) else 0):
                if m % 4 == 0:
                    memTb, rmem = getw([(0, 256, memT)])
                    wt, wr = getw([(0, 512, w_mk[l][:, (m // 4) * 512:(m // 4 + 1) * 512])])
                b = bank()

                def fn(e, wt=wt, b=b, m=m, memTb=memTb):
                    for k in range(KC):
                        ins = e.matmul(ps[:, b, 0:256], lhsT=wt[:, k, (m % 4) * 128:(m % 4 + 1) * 128], rhs=memTb[:, k, 0:256],
                                       start=(k == 0), stop=(k == KC - 1))
                    return ins
                P.op("pe", fn, reads=[wr, rmem], writes=[rps[b]])
                P.op("act", lambda e, b=b, m=m: e.activation(out=MKT[:, m, :], in_=ps[:, b, 0:256], func=AF.Copy), reads=[rps[b]], writes=[res("MKT")])
                if g == 0 and 'NOMKP' not in os.environ:
                    s_ = nxt("stg")
                    P.op("act", lambda e, b=b, s_=s_: e.activation(out=stg[s_][:, 0:256], in_=ps[:, b, 0:256], func=AF.Copy), reads=[rps[b]], writes=[rstg[s_]])
                    P.op("sp", lambda e, m=m, s_=s_: e.dma_start(out=mkp[l, m * 128:(m + 1) * 128, :], in_=stg[s_][:, 0:256]),
                         reads=[rstg[s_]], writes=[res("o_mkp")], dma="o_mk%d" % s_)
            for hf in range(2 if (KCROSS >= 2 and g == 0) else 0):
                memTb, rmem = getw([(0, 256, memT)])
                wt, wr = getw([(0, 512, w_mv[l][:, hf * 512:(hf + 1) * 512])])
                for mc in range(2):
                    b = bank()

                    def fn(e, wt=wt, b=b, mc=mc, memTb=memTb):
                        for k in range(KC):
                            ins = e.matmul(ps[:, b, :], lhsT=memTb[:, k, mc * 128:(mc + 1) * 128], rhs=wt[:, k, :],
                                           start=(k == 0), stop=(k == KC - 1))
                        return ins
                    P.op("pe", fn, reads=[wr, rmem], writes=[rps[b]])
                    P.op("act", lambda e, b=b, mc=mc, hf=hf: e.activation(out=MV[:, mc, hf * 512:(hf + 1) * 512], in_=ps[:, b, :], func=AF.Copy),
                         reads=[rps[b]], writes=[res("MV")])
                    if g == 0:
                        s_ = nxt("stg")
                        P.op("act", lambda e, b=b, s_=s_: e.activation(out=stg[s_][:, :], in_=ps[:, b, :], func=AF.Copy), reads=[rps[b]], writes=[rstg[s_]])
                        P.op("sp", lambda e, mc=mc, hf=hf, s_=s_: e.dma_start(out=mvp[l, mc * 128:(mc + 1) * 128, hf * 512:(hf + 1) * 512], in_=stg[s_][:, :]),
                             reads=[rstg[s_]], writes=[res("o_mvp")], dma="o_mk%d" % s_)
            if g == 0:
                P.op("sp", lambda e: [e.dma_start(out=scr_mkt[l], in_=MKT[:].rearrange("p c t -> p (c t)")),
                                      e.dma_start(out=scr_mv[l], in_=MV[:].rearrange("p c t -> p (c t)"))],
                     reads=[res("MKT"), res("MV")], writes=[res("scr_mk%d" % l)], dma="mkout", ndma=2)

            def cross_tile(t):
                c0, n = tile_cols(g, t)
                has_s = n > NPT
                wq = [getw([(0, 512, w_cq[l][:, 0:512])]), getw([(0, 512, w_cq[l][:, 512:1024])])]
                for m in range(KC):
                    wt, wr = wq[m // 4]
                    b = proj_chunk(wt, wr, (m % 4) * 128, g, t)
                    P.op("act", lambda e, b=b, m=m: e.activation(out=QcT[:, m, 0:n], in_=ps[:, b, 0:n], func=AF.Copy), reads=[rps[b]], writes=[res("QcT")])
                if has_s:
                    sample_cross(g, t, l)
                mx = small[:, 40:44]
                nb = small[:, 44:48]
                sm = small[:, 48:52]
                S2 = ps[:, 0:2, :].rearrange("p b (h c) -> p (b h) c", h=2)

                def ct_S(blk):
                    qs = slice(blk * 128, (blk + 1) * 128)

                    def fs(e):
                        for h in range(4):
                            o = ps[:, h // 2, (h % 2) * 256:(h % 2) * 256 + 256]
                            e.matmul(o, lhsT=QcT[:, 2 * h, qs], rhs=MKT[:, 2 * h, :], start=True, stop=False)
                            ins = e.matmul(o, lhsT=QcT[:, 2 * h + 1, qs], rhs=MKT[:, 2 * h + 1, :], start=False, stop=True)
                        return ins
                    P.op("pe", fs, reads=[res("QcT"), res("MKT")], writes=[rps[0], rps[1]])
                    state["bank"] = 2

                def ct_A1(blk):
                    P.op("dve", lambda e: e.tensor_reduce(out=mx, in_=S2, axis=AX.X, op=ALU.max), reads=[rps[0], rps[1]], writes=[res("smallC")])
                    P.op("dve", lambda e: e.tensor_scalar(out=nb, in0=mx, scalar1=-1.0 / 16.0, scalar2=None, op0=ALU.mult), reads=[res("smallC")], writes=[res("smallC")])

                def ct_A2(blk):
                    def fe(e):
                        for h in range(4):
                            ins = e.activation(out=Pm[:, h, :], in_=ps[:, h // 2, (h % 2) * 256:(h % 2) * 256 + 256], func=AF.Exp,
                                               bias=nb[:, h:h + 1], scale=1.0 / 16.0, accum_out=sm[:, h:h + 1])
                        return ins
                    P.op("act", fe, reads=[rps[0], rps[1], res("smallC")], writes=[res("Pm"), res("smallC2")])

                def ct_B(blk):
                    P.op("dve", lambda e: e.reciprocal(out=sm, in_=sm), reads=[res("smallC2")], writes=[res("smallC2")])

                    def fnorm(e):
                        for h in range(4):
                            ins = e.tensor_scalar(out=Pm[:, h, :], in0=Pm[:, h, :], scalar1=sm[:, h:h + 1], scalar2=None, op0=ALU.mult)
                        return ins
                    P.op("dve", fnorm, reads=[res("smallC2"), res("Pm")], writes=[res("Pm")])

                def ct_T(blk):
                    qs = slice(blk * 128, (blk + 1) * 128)

                    def ftr(e):
                        for h in range(4):
                            for mc in range(2):
                                ins = e.transpose(pst2[:, blk % 2, (h * 2 + mc) * 128:(h * 2 + mc + 1) * 128], Pm[:, h, mc * 128:(mc + 1) * 128], identb[:])
                        return ins
                    P.op("pe", ftr, reads=[res("Pm"), res("identb")], writes=[rpstl[blk % 2]])
                    P.op("act", lambda e: e.activation(out=PT[:, 0:8, :], in_=pst2[:, blk % 2, :].rearrange("p (a q) -> p a q", q=128), func=AF.Copy),
                         reads=[rpstl[blk % 2]], writes=[res("PT0")])
                    for hf in range(2):
                        def fpv(e, hf=hf):
                            for cc in range(4):
                                c = hf * 4 + cc
                                h = c // 2
                                for mc in range(2):
                                    ins = e.matmul(ps[:, 4 + hf, cc * 128:(cc + 1) * 128], lhsT=MV[:, mc, c * 128:(c + 1) * 128], rhs=PT[:, h * 2 + mc, :],
                                                   start=(mc == 0), stop=(mc == 1))
                            return ins
                        P.op("pe", fpv, reads=[res("MV"), res("PT0")], writes=[rps[4 + hf]])
                        P.op("act", lambda e, hf=hf: e.activation(out=Zb[:, hf * 4:hf * 4 + 4, qs],
                                                                  in_=ps[:, 4 + hf, :].rearrange("p (c q) -> p c q", q=128), func=AF.Copy),
                             reads=[rps[4 + hf]], writes=[res("Zb")])

                nblk = (n - (NS if has_s else 0)) // 128
                if KCROSS >= 4:
                    ct_S(0)
                    ct_A1(0)
                    ct_A2(0)
                    for blk in range(nblk):
                        if blk + 1 < nblk:
                            ct_S(blk + 1)
                        ct_B(blk)
                        if blk + 1 < nblk:
                            ct_A1(blk + 1)
                        ct_T(blk)
                        if blk + 1 < nblk:
                            ct_A2(blk + 1)
                if KCROSS < 5:
                    return
                w_o = [getw([(0, 512, w_co[l][:, 0:512])]), getw([(0, 512, w_co[l][:, 512:1024])])]
                for m in range(KC):
                    wt, wr = w_o[m // 4]
                    b = proj_chunk(wt, wr, (m % 4) * 128, g, t, rhs_src=Zb, rhs_res=res("Zb"))
                    resid_evac(g, t, m, b)
                layer_norm(g, t, (l, 2))
            state["trim"] = {0: 128, 1: 256}[l] if g == 0 else 0
            try:
                for t in range(NT if KCROSS >= 3 else 0):
                    cross_tile(t)
            finally:
                state["trim"] = 0

        def sample_cross(g, t, l):
            c0, n = tile_cols(g, t)
            tok_major(g, t, [[(0, 512, w_cq[l][:, 0:512])], [(0, 512, w_cq[l][:, 512:1024])]], [512, 512], 0)

            def f1(e):
                src = tokS[:, 0:1024].rearrange("b (h d) -> b h d", d=256)
                return [e.dma_start(out=scr_qc[:, :, 0, :], in_=src), e.dma_start(out=scr_qc[:, :, 1, :], in_=src)]
            P.op("sp", f1, reads=[res("tokS")], writes=[res("scr_qc")], dma="scc1", ndma=2)
            P.op("sp", lambda e: e.dma_start(out=qx[:, :], in_=scr_qc.rearrange("b h m d -> (b h m) d")),
                 reads=[res("scr_qc")], writes=[res("qx")], dma="scc2")
            sc = ssm[:, 0:128]
            for i in range(8):
                s_ = i % 2
                P.op("sp", lambda e, i=i, s_=s_: e.dma_start(out=SBh[s_][:, :, :], in_=cmk[l][:, i * 4096:(i + 1) * 4096].rearrange("p (m d) -> p m d", d=256)),
                     writes=[rSB[s_]], dma="sbh%d" % s_, arena=True)
                P.op("dve", lambda e, s_=s_: e.tensor_tensor(out=SBh[s_][:, :, :], in0=SBh[s_][:, :, :],
                                                             in1=qx[:, :].unsqueeze(1).to_broadcast([128, 16, 256]), op=ALU.mult),
                     reads=[rSB[s_], res("qx")], writes=[rSB[s_]])
                P.op("dve", lambda e, s_=s_, i=i: e.tensor_reduce(out=sc[:, i * 16:(i + 1) * 16], in_=SBh[s_][:, :, :], axis=AX.X, op=ALU.add),
                     reads=[rSB[s_]], writes=[res("ssm")])
            P.op("sp", lambda e: e.dma_start(out=scr_sc, in_=sc), reads=[res("ssm")], writes=[res("scr_sc")], dma="scc3")
            s2 = ssm[0:64, 0:256]
            P.op("sp", lambda e: e.dma_start(out=s2, in_=scr_sc.rearrange("(a mh) m -> a (mh m)", mh=2)), reads=[res("scr_sc")], writes=[res("ssm")], dma="scc4")
            mx = small[0:64, 56:57]
            nb = small[0:64, 57:58]
            sm = small[0:64, 58:59]
            P.op("dve", lambda e: e.tensor_reduce(out=mx, in_=s2, axis=AX.X, op=ALU.max), reads=[res("ssm")], writes=[res("smallX")])
            P.op("dve", lambda e: e.tensor_scalar(out=nb, in0=mx, scalar1=-1.0 / 16.0, scalar2=None, op0=ALU.mult), reads=[res("smallX")], writes=[res("smallX")])
            P.op("act", lambda e: e.activation(out=s2, in_=s2, func=AF.Exp, bias=nb, scale=1.0 / 16.0), reads=[res("ssm"), res("smallX")], writes=[res("ssm")])
            P.op("dve", lambda e: e.tensor_reduce(out=sm, in_=s2, axis=AX.X, op=ALU.add), reads=[res("ssm")], writes=[res("smallX")])
            P.op("dve", lambda e: e.reciprocal(out=sm, in_=sm), reads=[res("smallX")], writes=[res("smallX")])
            P.op("dve", lambda e: e.tensor_scalar(out=s2, in0=s2, scalar1=sm, scalar2=None, op0=ALU.mult), reads=[res("ssm"), res("smallX")], writes=[res("ssm")])
            P.op("sp", lambda e: e.dma_start(out=scr_pc.rearrange("(a mh) m -> a (mh m)", mh=2), in_=s2), reads=[res("ssm")], writes=[res("scr_pc")], dma="scc5")
            pc = ssm[:, 128:256]
            P.op("sp", lambda e: e.dma_start(out=pc, in_=scr_pc), reads=[res("scr_pc")], writes=[res("ssm")], dma="scc6")
            for i in range(8):
                s_ = i % 2
                P.op("sp", lambda e, i=i, s_=s_: e.dma_start(out=SBh[s_][:, :, :], in_=cmv[l][:, i * 4096:(i + 1) * 4096].rearrange("p (m d) -> p m d", d=256)),
                     writes=[rSB[s_]], dma="sbh%d" % s_, arena=True)
                P.op("dve", lambda e, s_=s_, i=i: e.tensor_tensor(out=SBh[s_][:, :, :], in0=SBh[s_][:, :, :],
                                                                  in1=pc[:, i * 16:(i + 1) * 16].unsqueeze(2).to_broadcast([128, 16, 256]), op=ALU.mult),
                     reads=[rSB[s_], res("ssm")], writes=[rSB[s_]])
                if i == 0:
                    P.op("dve", lambda e, s_=s_: e.tensor_reduce(out=oacc[:, :], in_=SBh[s_][:, :, :].rearrange("p m d -> p d m"), axis=AX.X, op=ALU.add),
                         reads=[rSB[s_]], writes=[res("oacc")])
                else:
                    P.op("dve", lambda e, s_=s_: e.tensor_reduce(out=qx[:, :], in_=SBh[s_][:, :, :].rearrange("p m d -> p d m"), axis=AX.X, op=ALU.add),
                         reads=[rSB[s_]], writes=[res("qx")])
                    P.op("dve", lambda e: e.tensor_tensor(out=oacc[:, :], in0=oacc[:, :], in1=qx[:, :], op=ALU.add),
                         reads=[res("qx"), res("oacc")], writes=[res("oacc")])
            P.op("sp", lambda e: e.dma_start(out=scr_oc, in_=oacc[:, :]), reads=[res("oacc")], writes=[res("scr_oc")], dma="scc7")
            P.op("sp", lambda e: e.dma_start(out=tokS[:, 0:2048], in_=scr_oc.rearrange("(b x) d -> b (x d)", b=NS)),
                 reads=[res("scr_oc")], writes=[res("tokS")], dma="scc8", arena=True)
            tv = tokS[:, 0:2048].rearrange("b (h m d) -> b h m d", h=4, m=2)
            P.op("dve", lambda e: e.tensor_tensor(out=tv[:, :, 0, :], in0=tv[:, :, 0, :], in1=tv[:, :, 1, :], op=ALU.add),
                 reads=[res("tokS")], writes=[res("tokS")])
            def ftr(e):
                for c in range(8):
                    h, dc = c // 2, c % 2
                    ins = e.transpose(ps[:, 4, c * NS:(c + 1) * NS], tv[:, h, 0, dc * 128:(dc + 1) * 128], identf[0:NS, 0:NS])
                return ins
            P.op("pe", ftr, reads=[res("tokS"), res("identf")], writes=[rps[4]])
            P.op("act", lambda e: e.activation(out=Zb[:, 0:8, NPT:n], in_=ps[:, 4, 0:8 * NS].rearrange("p (c s) -> p c s", s=NS), func=AF.Copy),
                 reads=[rps[4]], writes=[res("Zb")])

        nstep = 0
        for g in range(NG):
            gc0 = g * GP
            for t in range(NT):
                c0, n = tile_cols(g, t)

                def fx(e, c0=c0, n=n, gc0=gc0):
                    return e.dma_start(out=X[:, :, c0:c0 + n], in_=xT[:, gc0 + c0:gc0 + c0 + n].rearrange("(k p) c -> p k c", p=128))
                P.op("sp", fx, writes=rX[t] + [], dma="xin%d" % t)
                P.op("dve", lambda e, c0=c0, n=n: e.tensor_copy(out=Xb[:, :, c0:c0 + n], in_=X[:, :, c0:c0 + n]), reads=rX[t], writes=[rXb[t]])
            for l in range(L):
                for si, stage in enumerate((lambda: ffn(g, l, 0), lambda: mix(g, l), lambda: cross(g, l), lambda: ffn(g, l, 1))):
                    if nstep < KSTOP:
                        if si > 0:
                            P.barrier()
                        stage()
                    nstep += 1
            drain(10 ** 6)
            for t in range(NT):
                c0, n = tile_cols(g, t)
                gl = gc0 + c0
                lo = max(gl, HALO)
                hi = gl + NPT
                if hi > lo:
                    def fo(e, lo=lo, hi=hi, gl=gl, c0=c0):
                        return e.dma_start(out=yT[:, lo - HALO:hi - HALO].rearrange("(k p) c -> p k c", p=128),
                                           in_=X[:, :, c0 + (lo - gl):c0 + (hi - gl)])
                    P.op("sp", fo, reads=rX[t], writes=[res("o_y")], dma="yout")
                if n > NPT:
                    def fo2(e, c0=c0, n=n):
                        return e.dma_start(out=yT[:, OWN:OWN + NS].rearrange("(k p) c -> p k c", p=128), in_=X[:, :, c0 + NPT:c0 + n])
                    P.op("sp", fo2, reads=rX[t], writes=[res("o_y")], dma="yout")
            P.barrier()

    run_build(Prog(), None)
    reqs = list(state["rec"])
    P = Prog()
    run_build(P, reqs)
    P.emit(nc)
    if 'KVERB' in os.environ:
        print('SEMCOUNTS', {k: v for k, v in P.cnt.items()})
    return nc


_NC = None


def _prep(inputs):
    f = lambda a: np.ascontiguousarray(np.asarray(a), dtype=np.float32)
    xp = f(inputs["x_prompt"]); xs = f(inputs["x_sample"]); mem = f(inputs["mem_prompt"])
    cwk = f(inputs["cache_win_k"]); cwv = f(inputs["cache_win_v"]); stc = f(inputs["state_conv"])
    cmk = f(inputs["cache_mem_k"]); cmv = f(inputs["cache_mem_v"])
    ln_g = f(inputs["ln_g"]); ln_b = f(inputs["ln_b"])
    shared = {
        "w_gu1": f(inputs["ffn1_w_gu"]), "w_gu2": f(inputs["ffn2_w_gu"]),
        "w_dn1": f(inputs["ffn1_w_down"]), "w_dn2": f(inputs["ffn2_w_down"]),
        "w_in": f(inputs["w_in"]), "w_out": f(inputs["w_out"]), "w_cq": f(inputs["w_cq"]),
        "w_mk": f(inputs["w_mk"]), "w_mv": f(inputs["w_mv"]), "w_co": f(inputs["w_co"]),
    }
    gvec = np.zeros((128, L * 4 * 8 * 2), np.float32)
    for l in range(L):
        for i in range(4):
            for k in range(8):
                gvec[:, gcol(l, i, k)] = ln_g[l, i, k * 128:(k + 1) * 128]
                gvec[:, gcol(l, i, k) + 1] = ln_b[l, i, k * 128:(k + 1) * 128]
    cwt = f(inputs["conv_w"])
    convw = np.zeros((128, L * 3 * 4), np.float32)
    for l in range(L):
        for tap in range(3):
            for c in range(4):
                convw[:, (l * 3 + tap) * 4 + c] = cwt[l, tap, c * 128:(c + 1) * 128]
    snk = f(inputs["attn_sinks"])
    sinkq = np.ascontiguousarray(np.broadcast_to(snk.reshape(1, L * 8), (128, L * 8)))
    sinks = np.ascontiguousarray(np.stack([snk[l][np.arange(128) % 8] for l in range(L)], axis=1))
    ident = np.eye(128, dtype=np.float32)
    a = np.arange(128)[:, None]; c = np.arange(256)[None, :]
    maskA = np.where((c >= a) & (c <= a + 128), 0.0, NEG).astype(np.float32)
    maskF0 = maskA.copy(); maskF0[:, :128] = NEG
    shared.update(gvec=gvec, convw=convw, sinkq=sinkq, sinks=sinks, ident=ident, maskA=maskA)
    in_maps = []
    for core in range(8):
        b, ch = core // 4, core % 4
        start = ch * OWN
        seg = np.zeros((HALO + OWN, D), np.float32)
        lo = start - HALO
        if lo < 0:
            seg[HALO:] = xp[b, 0:OWN]
        else:
            seg[:] = xp[b, lo:start + OWN]
        s0 = core * NS
        xT = np.ascontiguousarray(np.concatenate([seg, xs[s0:s0 + NS, 0, :]], axis=0).T)
        m = dict(shared)
        m["xT"] = xT
        m["memT"] = np.ascontiguousarray(mem[b].T)
        m["maskF"] = maskF0 if ch == 0 else maskA
        m["flag"] = np.full((128, 1), 0.0 if ch == 0 else 1.0, np.float32)
        st = stc[:, s0:s0 + NS]
        st = st.reshape(L, NS, 2, 4, 128).transpose(4, 0, 3, 2, 1)
        m["stT"] = np.ascontiguousarray(st.reshape(128, L * 4 * 2 * NS))
        m["wk_o"] = np.ascontiguousarray(cwk[:, s0:s0 + NS].reshape(L, NS, 128, 128))
        m["wv_o"] = np.ascontiguousarray(cwv[:, s0:s0 + NS].reshape(L, NS, 128, 128))
        for nm, arr in (("wk_s", cwk), ("wv_s", cwv)):
            w_ = arr[:, s0:s0 + NS]
            w_ = w_.transpose(0, 1, 3, 2, 4)
            w_ = np.repeat(w_[:, :, :, None], 4, axis=3)
            m[nm] = np.ascontiguousarray(w_.reshape(L, 128, 8192))
        for nm, arr in (("cmk", cmk), ("cmv", cmv)):
            w_ = arr[:, s0:s0 + NS]
            w_ = w_.reshape(L, NS, 2, 128, 4, 256).transpose(0, 1, 4, 2, 3, 5)
            m[nm] = np.ascontiguousarray(w_.reshape(L, 128, 32768))
        in_maps.append(m)
    return in_maps


def kernel(**inputs):
    global _NC
    if _NC is None:
        _NC = build_nc()
    in_maps = _prep(inputs)
    res = run_bass_kernel_spmd(_NC, in_maps, core_ids=list(range(8)))
    R_ = res.results
    B, S = 2, 8192
    yp = np.zeros((B, S, D), np.float32)
    ys = np.zeros((128, 1, D), np.float32)
    wkp = np.zeros((L, B, 128, 2, 64), np.float32); wvp = np.zeros_like(wkp)
    cvp = np.zeros((L, B, 2, 512), np.float32)
    mkp = np.zeros((L, B, 256, 4, 256), np.float32); mvp = np.zeros_like(mkp)
    wks = np.zeros((L, 128, 128, 2, 64), np.float32); wvs = np.zeros_like(wks)
    cvs = np.zeros((L, 128, 2, 512), np.float32)
    for core in range(8):
        r = R_[core]
        b, ch = core // 4, core % 4
        s0 = core * NS
        yT = np.asarray(r["yT"])
        yp[b, ch * OWN:(ch + 1) * OWN] = yT[:, 0:OWN].T
        ys[s0:s0 + NS, 0] = yT[:, OWN:OWN + NS].T
        wks[:, s0:s0 + NS] = np.asarray(r["wks"]).reshape(L, NS, 128, 2, 64)
        wvs[:, s0:s0 + NS] = np.asarray(r["wvs"]).reshape(L, NS, 128, 2, 64)
        cs = np.asarray(r["convs"])
        cvs[:, s0:s0 + NS] = cs.transpose(0, 4, 3, 1, 2).reshape(L, NS, 2, 512)
        if ch == 3:
            wkp[:, b] = np.asarray(r["wkp"]).transpose(0, 2, 1).reshape(L, 128, 2, 64)
            wvp[:, b] = np.asarray(r["wvp"]).reshape(L, 128, 2, 64)
            cp = np.asarray(r["convp"])
            cvp[:, b] = cp.transpose(0, 3, 1, 2).reshape(L, 2, 512)
        if ch == 0:
            mkp[:, b] = np.asarray(r["mkp"]).transpose(0, 2, 1).reshape(L, 256, 4, 256)
            mvp[:, b] = np.asarray(r["mvp"]).reshape(L, 256, 4, 256)
    return (yp, ys, wkp, wvp, cvp, mkp, mvp, wks, wvs, cvs)
```
